# Optimizing a Trainium2 kernel written in Bass

```python
import math
import jax, jax.numpy as jnp
from jax import lax
import numpy as np

D_MODEL = 1024
BATCH = 4
SEQ = 8192
DEPTH = 2

HEAD_DIM = 64
ROT_DIM = HEAD_DIM // 4
ROPE_THETA = 500000.0
BLK = 128
NEG_INF = -1e30
EPS = 1e-6
A_HEADS = 8
A_CONFIGS = ((128, 1), (512, 4), (2048, 16))
B_Q_HEADS = 8
B_KV_HEADS = 2
B_GROUP = B_Q_HEADS // B_KV_HEADS
B_WINDOW = 128
C_QK_HEADS = 4
C_V_HEADS = 8
C_DK = 128
C_DV = 128
C_CONV = 4
C_CHUNK = 64
D_FF = 4 * D_MODEL
A_W = A_HEADS * HEAD_DIM
B_QW = B_Q_HEADS * HEAD_DIM
B_KVW = B_KV_HEADS * HEAD_DIM
C_QKW = C_QK_HEADS * C_DK
C_VW = C_V_HEADS * C_DV
IN_WIDTHS = (A_W, A_W, A_W, B_QW, B_KVW, B_KVW, C_QKW, C_QKW, C_VW, C_VW,
             C_V_HEADS, C_V_HEADS, D_MODEL, D_MODEL, D_MODEL)
D_IN = sum(IN_WIDTHS)
MAX_POS_OFFSET = 4096

kernel_name = "hybrid_gated_dilated_swa_deltanet_block"


def rmsnorm(x, gain):
    xf = x.astype(jnp.float32)
    y = xf * lax.rsqrt(jnp.mean(xf * xf, axis=-1, keepdims=True) + EPS)
    return (y * gain.astype(jnp.float32)).astype(x.dtype)


def l2norm(t):
    tf = t.astype(jnp.float32)
    return tf * lax.rsqrt(jnp.sum(tf * tf, axis=-1, keepdims=True) + EPS)


def rope_tables(positions, dtype):
    inv_freq = jnp.power(ROPE_THETA, -jnp.arange(0, ROT_DIM, 2, dtype=jnp.float32) / ROT_DIM)
    ang = positions.astype(jnp.float32)[..., None] * inv_freq
    return jnp.cos(ang)[:, :, None, :].astype(dtype), jnp.sin(ang)[:, :, None, :].astype(dtype)


def apply_rope(x, cos, sin):
    half = ROT_DIM // 2
    x1, x2 = x[..., :half], x[..., half:ROT_DIM]
    return jnp.concatenate([x1 * cos - x2 * sin, x2 * cos + x1 * sin, x[..., ROT_DIM:]], axis=-1)


def split_columns(u):
    outs, start = [], 0
    for width in IN_WIDTHS:
        outs.append(u[..., start:start + width])
        start += width
    return outs


def banded_attention(q, k, v, max_dist, sink=None):
    b, l, hkv, g, d = q.shape
    nb = l // BLK
    qb = q.reshape(b, nb, BLK, hkv, g, d)

    def with_prev(t):
        tb = t.reshape(b, nb, BLK, hkv, d)
        prev = jnp.concatenate([jnp.zeros_like(tb[:, :1]), tb[:, :-1]], axis=1)
        return jnp.concatenate([prev, tb], axis=2)

    kk, vv = with_prev(k), with_prev(v)
    s = jnp.einsum('bnqhgd,bnkhd->bnhgqk', qb, kk,
                   preferred_element_type=jnp.float32) * (d ** -0.5)
    qi = jnp.arange(BLK)[:, None]
    kj = jnp.arange(2 * BLK)[None, :]
    dist = BLK + qi - kj
    band = (dist >= 0) & (dist <= max_dist)
    not_pad = (jnp.arange(nb) > 0)[:, None, None] | (kj >= BLK)[None]
    valid = band[None] & not_pad
    s = jnp.where(valid[None, :, None, None], s, NEG_INF)
    m = jnp.max(s, axis=-1, keepdims=True)
    if sink is not None:
        sk = sink.astype(jnp.float32)[None, None, :, :, None, None]
        m = jnp.maximum(m, sk)
    p = jnp.exp(s - m)
    den = jnp.sum(p, axis=-1, keepdims=True)
    if sink is not None:
        den = den + jnp.exp(sk - m)
    o = jnp.einsum('bnhgqk,bnkhd->bnqhgd', p.astype(v.dtype), vv,
                   preferred_element_type=jnp.float32)
    den_t = jnp.transpose(den[..., 0], (0, 1, 4, 2, 3))
    lse_t = jnp.transpose((m + jnp.log(den))[..., 0], (0, 1, 4, 2, 3))
    o = (o / den_t[..., None]).reshape(b, l, hkv, g, d)
    return o.astype(q.dtype), lse_t.reshape(b, l, hkv, g)


def dilated_attention(q, k, v):
    b, s, h, d = q.shape
    outs, lses = [], []
    for window, dil in A_CONFIGS:
        steps = window // dil
        l = s // dil
        lp = -(-l // BLK) * BLK

        def by_stride(t):
            t = t.reshape(b, l, dil, h, d).transpose(0, 2, 1, 3, 4).reshape(b * dil, l, h, d)
            return jnp.pad(t, ((0, 0), (0, lp - l), (0, 0), (0, 0)))

        o, lse = banded_attention(by_stride(q)[:, :, :, None], by_stride(k), by_stride(v), steps)
        o = o[:, :l, :, 0].reshape(b, dil, l, h, d).transpose(0, 2, 1, 3, 4).reshape(b, s, h, d)
        lse = lse[:, :l, :, 0].reshape(b, dil, l, h).transpose(0, 2, 1, 3).reshape(b, s, h)
        outs.append(o)
        lses.append(lse)
    w = jax.nn.softmax(jnp.stack(lses, axis=0), axis=0)
    o = jnp.sum(w[..., None] * jnp.stack(outs, axis=0).astype(jnp.float32), axis=0)
    return o.astype(q.dtype)


def short_conv(x, w):
    s = x.shape[1]
    xp = jnp.pad(x, ((0, 0), (C_CONV - 1, 0), (0, 0)))
    y = xp[:, 0:s] * w[0]
    for j in range(1, C_CONV):
        y = y + xp[:, j:j + s] * w[j]
    return jax.nn.silu(y)


def gated_delta_rule(q, k, v, g, beta):
    b, s, h, dk = k.shape
    dv = v.shape[-1]
    nc = s // C_CHUNK

    def chunks(t):
        t = t.astype(jnp.float32).reshape((b, nc, C_CHUNK) + t.shape[2:])
        return jnp.moveaxis(t, 3, 1)

    qc = chunks(q) * (dk ** -0.5)
    kc, vc, bc = chunks(k), chunks(v), chunks(beta)
    gc = jnp.cumsum(chunks(g), axis=-1)
    tril = jnp.tril(jnp.ones((C_CHUNK, C_CHUNK), dtype=bool))
    strict = jnp.tril(jnp.ones((C_CHUNK, C_CHUNK), dtype=bool), -1)
    diff = gc[..., :, None] - gc[..., None, :]
    decay = jnp.where(tril, jnp.exp(jnp.where(tril, diff, 0.0)), 0.0)
    kkt = jnp.einsum('bhncd,bhnjd->bhncj', kc, kc)
    a_mat = jnp.where(strict, kkt * decay * bc[..., None], 0.0)
    eye = jnp.eye(C_CHUNK, dtype=jnp.float32)
    rhs = jnp.concatenate([vc * bc[..., None], kc * (bc * jnp.exp(gc))[..., None]], axis=-1)
    sol = lax.linalg.triangular_solve(a_mat + eye, rhs, left_side=True, lower=True,
                                      unit_diagonal=True)
    u, w = sol[..., :dv], sol[..., dv:]
    attn = jnp.where(tril, jnp.einsum('bhncd,bhnjd->bhncj', qc, kc) * decay, 0.0)
    q_dec = qc * jnp.exp(gc)[..., None]
    k_dec = kc * jnp.exp(gc[..., -1:] - gc)[..., None]
    g_last = jnp.exp(gc[..., -1])

    def step(state, xs):
        u_i, w_i, attn_i, qd_i, kd_i, gl_i = xs
        v_new = u_i - jnp.einsum('bhcd,bhde->bhce', w_i, state)
        o_i = (jnp.einsum('bhcd,bhde->bhce', qd_i, state)
               + jnp.einsum('bhcj,bhje->bhce', attn_i, v_new))
        state = state * gl_i[..., None, None] + jnp.einsum('bhcd,bhce->bhde', kd_i, v_new)
        return state, o_i

    xs = tuple(jnp.moveaxis(t, 2, 0) for t in (u, w, attn, q_dec, k_dec, g_last))
    state0 = jnp.zeros((b, h, dk, dv), jnp.float32)
    _, o = lax.scan(step, state0, xs)
    return jnp.transpose(o, (1, 0, 3, 2, 4)).reshape(b, s, h, dv)


def mixer_block(h, cos, sin, w_in, b_in, conv_w, a_log, dt_bias, sinks, c_norm,
                w_branch_a, w_branch_b, w_branch_c, w_out):
    b, s, _ = h.shape
    u = h @ w_in + b_in
    (a_q, a_k, a_v, b_q, b_k, b_v, c_q, c_k, c_v, c_z, c_a, c_b,
     gate_a, gate_b, gate_c) = split_columns(u)

    def heads(t, n):
        return t.reshape(b, s, n, -1)

    ya = dilated_attention(apply_rope(heads(a_q, A_HEADS), cos, sin),
                           apply_rope(heads(a_k, A_HEADS), cos, sin),
                           heads(a_v, A_HEADS)).reshape(b, s, A_W)

    qb = apply_rope(heads(b_q, B_Q_HEADS), cos, sin).reshape(b, s, B_KV_HEADS, B_GROUP, HEAD_DIM)
    kb = apply_rope(heads(b_k, B_KV_HEADS), cos, sin)
    vb = heads(b_v, B_KV_HEADS)
    yb, _ = banded_attention(qb, kb, vb, B_WINDOW - 1, sink=sinks.reshape(B_KV_HEADS, B_GROUP))
    yb = yb.reshape(b, s, B_QW)

    qkv = short_conv(jnp.concatenate([c_q, c_k, c_v], axis=-1), conv_w)
    rep = C_V_HEADS // C_QK_HEADS
    cq = jnp.repeat(l2norm(qkv[..., :C_QKW].reshape(b, s, C_QK_HEADS, C_DK)), rep, axis=2)
    ck = jnp.repeat(l2norm(qkv[..., C_QKW:2 * C_QKW].reshape(b, s, C_QK_HEADS, C_DK)), rep, axis=2)
    cv = qkv[..., 2 * C_QKW:].reshape(b, s, C_V_HEADS, C_DV)
    beta = jax.nn.sigmoid(c_b.astype(jnp.float32))
    g = -jnp.exp(a_log.astype(jnp.float32)) * jax.nn.softplus(
        c_a.astype(jnp.float32) + dt_bias.astype(jnp.float32))
    yc = gated_delta_rule(cq, ck, cv, g, beta)
    yc = rmsnorm(yc, c_norm) * jax.nn.silu(c_z.reshape(b, s, C_V_HEADS, C_DV).astype(jnp.float32))
    yc = yc.astype(h.dtype).reshape(b, s, C_VW)

    merged = (jax.nn.sigmoid(gate_a) * (ya @ w_branch_a)
              + jax.nn.sigmoid(gate_b) * (yb @ w_branch_b)
              + jax.nn.sigmoid(gate_c) * (yc @ w_branch_c))
    return merged @ w_out


def setup_inputs(seed: int = 0) -> dict:
    key = jax.random.key(seed)
    ks = jax.random.split(key, 20)
    f32 = jnp.float32

    def dense(k, shape, fan_in, scale=1.0):
        return jax.random.normal(k, shape, f32) * (scale * fan_in ** -0.5)

    def gain(k, shape):
        return 1.0 + 0.05 * jax.random.normal(k, shape, f32)

    res_scale = (2 * DEPTH) ** -0.5
    x = jax.random.normal(ks[0], (BATCH, SEQ, D_MODEL), f32)
    positions = (jax.random.randint(ks[1], (BATCH, 1), 0, MAX_POS_OFFSET, dtype=jnp.int32)
                 + jnp.arange(SEQ, dtype=jnp.int32)[None, :])
    norm_mix = gain(ks[2], (DEPTH, D_MODEL))
    w_in = dense(ks[3], (DEPTH, D_MODEL, D_IN), D_MODEL)
    b_in = 0.02 * jax.random.normal(ks[4], (DEPTH, D_IN), f32)
    conv_w = dense(ks[5], (DEPTH, C_CONV, 2 * C_QKW + C_VW), C_CONV)
    a_log = jnp.log(jax.random.uniform(ks[6], (DEPTH, C_V_HEADS), f32, 1.0, 16.0))
    dt = jnp.exp(jax.random.uniform(ks[7], (DEPTH, C_V_HEADS), f32, math.log(1e-3), math.log(1e-1)))
    dt_bias = dt + jnp.log(-jnp.expm1(-dt))
    sinks = 0.5 * jax.random.normal(ks[8], (DEPTH, B_Q_HEADS), f32)
    c_norm = gain(ks[9], (DEPTH, C_DV))
    w_branch_a = dense(ks[10], (DEPTH, A_W, D_MODEL), A_W)
    w_branch_b = dense(ks[11], (DEPTH, B_QW, D_MODEL), B_QW)
    w_branch_c = dense(ks[12], (DEPTH, C_VW, D_MODEL), C_VW)
    w_out = dense(ks[13], (DEPTH, D_MODEL, D_MODEL), D_MODEL, res_scale)
    norm_ffn = gain(ks[14], (DEPTH, D_MODEL))
    w_ff1 = dense(ks[15], (DEPTH, D_MODEL, D_FF), D_MODEL)
    w_ff2 = dense(ks[16], (DEPTH, D_FF, D_MODEL), D_FF, res_scale)
    norm_final = gain(ks[17], (D_MODEL,))
    return {"x": x, "positions": positions, "norm_mix": norm_mix, "w_in": w_in, "b_in": b_in,
            "conv_w": conv_w, "a_log": a_log, "dt_bias": dt_bias, "sinks": sinks, "c_norm": c_norm,
            "w_branch_a": w_branch_a, "w_branch_b": w_branch_b, "w_branch_c": w_branch_c,
            "w_out": w_out, "norm_ffn": norm_ffn, "w_ff1": w_ff1, "w_ff2": w_ff2,
            "norm_final": norm_final}


def reference(x, positions, norm_mix, w_in, b_in, conv_w, a_log, dt_bias, sinks, c_norm,
              w_branch_a, w_branch_b, w_branch_c, w_out, norm_ffn, w_ff1, w_ff2, norm_final):
    cos, sin = rope_tables(positions, x.dtype)
    for layer in range(DEPTH):
        h = rmsnorm(x, norm_mix[layer])
        x = x + mixer_block(h, cos, sin, w_in[layer], b_in[layer], conv_w[layer], a_log[layer],
                            dt_bias[layer], sinks[layer], c_norm[layer], w_branch_a[layer],
                            w_branch_b[layer], w_branch_c[layer], w_out[layer])
        h = rmsnorm(x, norm_ffn[layer])
        x = x + jnp.square(jax.nn.relu(h @ w_ff1[layer])) @ w_ff2[layer]
    return rmsnorm(x, norm_final)
```

```python
import numpy as np
import concourse.bass as bass
import concourse.mybir as mybir
from contextlib import ExitStack

F32 = mybir.dt.float32
BF16 = mybir.dt.bfloat16
I32 = mybir.dt.int32
AF = mybir.ActivationFunctionType
ALU = mybir.AluOpType
AX = mybir.AxisListType

ENGS = ("pe", "dve", "act", "pool", "sp")
NDSEM = 8


class Tile:
    __slots__ = ("name", "h", "space", "fs", "acc", "whole")

    def __init__(self, name, h, space, fs, whole=False):
        self.name, self.h, self.space, self.fs = name, h, space, fs
        self.acc = []
        self.whole = whole

    def __getitem__(self, idx):
        return V(self, self.h[idx])

    def ap(self):
        return V(self, self.h[:])


class V:
    __slots__ = ("t", "ap")

    def __init__(self, t, ap):
        self.t, self.ap = t, ap

    def box(self):
        t = self.t
        if t.whole:
            return (0, 127, 0, 1 << 30)
        a = self.ap
        off = a.offset
        pat = a.ap
        fs = t.fs
        plo = off // fs
        flo = off % fs
        pstep, pcnt = pat[0]
        phi = plo + (pcnt - 1) * (pstep // fs if fs else 0)
        fhi = flo
        for st, cn in pat[1:]:
            if st > 0:
                fhi += (cn - 1) * st
        return (plo, phi, flo, fhi)

    def re(self, pattern, **kw):
        return V(self.t, self.ap.rearrange(pattern, **kw))

    def bc(self, shape):
        return V(self.t, self.ap.to_broadcast(shape))

    def __getitem__(self, idx):
        return V(self.t, self.ap[idx])


class Op:
    __slots__ = ("eng", "fn", "waits", "idx", "seq", "dsem", "dval", "isdma")


class K:
    def __init__(self):
        self.nc = bass.Bass("TRN2", target_bir_lowering=False)
        self.ops = []
        self.es = ExitStack()
        self.tiles = []
        self.nseq = {e: 0 for e in ENGS}
        self.ndma = {e: 0 for e in ENGS}
        self.waited = {e: {} for e in ENGS}
        self.dma_hist = {e: [] for e in ENGS}
        self.drams = {}
        self.pend = {}
        self.nosame = False

    def dram(self, name, shape, dt, kind="Internal"):
        h = self.nc.dram_tensor(name, list(shape), dt, kind=kind)
        self.drams[name] = h
        t = Tile(name, h, "dram", 0)
        return t

    def arena(self, nbytes):
        self.arena_h = self.es.enter_context(self.nc.sbuf_tensor("arena", [128, nbytes // 4], F32))
        self.arena_n = nbytes
        self.bump = 0
        self.mark = 0

    def sb(self, name, shape, dt):
        esz = {F32: 4, BF16: 2, I32: 4}[dt]
        n = int(np.prod(shape[1:]))
        nb = (n * esz + 31) // 32 * 32
        off = self.bump
        assert off + nb <= self.arena_n, ("arena overflow", name, off, nb, self.arena_n)
        self.bump = off + nb
        base = self.arena_h[0:shape[0], off // 4:(off + nb) // 4]
        if dt != F32:
            base = base.bitcast(dt)
        base = base[:, 0:n]
        if len(shape) == 3:
            base = base.rearrange("p (a b) -> p a b", a=shape[1])
        elif len(shape) == 4:
            base = base.rearrange("p (a b c) -> p a b c", a=shape[1], b=shape[2])
        t = Tile(name, base, "sb", (self.arena_n // esz))
        self.tiles.append(t)
        return t

    def set_mark(self):
        self.mark = self.bump

    def phase_reset(self):
        self.bump = self.mark
        self.barrier()

    def barrier(self):
        cur = {}
        for e in ENGS:
            if self.nseq[e] > 0:
                cur[("e", e)] = self.nseq[e]
            n = self.ndma[e]
            for kk in range(min(n, NDSEM)):
                cur[("d", e, kk)] = 16 * ((n - 1 - kk) // NDSEM + 1)
        for e in ENGS:
            pb = self.pend.setdefault(e, {})
            for key, val in cur.items():
                if key == ("e", e):
                    continue
                if pb.get(key, 0) < val:
                    pb[key] = val

    def ps(self, name, shape, dt=F32):
        h = self.es.enter_context(self.nc.psum_tensor(name, list(shape), dt))
        fs = int(np.prod(shape[1:]))
        t = Tile(name, h, "ps", fs, whole=True)
        self.tiles.append(t)
        return t

    def _deps(self, reads, writes):
        deps = set()
        for (v, w) in [(r, False) for r in reads] + [(x, True) for x in writes]:
            t = v.t
            if t.space == "dram":
                bx = self._drambox(v)
            else:
                bx = v.box()
            excl = w or t.whole
            newacc = []
            for a in t.acc:
                (oi, aw, plo, phi, flo, fhi) = a
                ov = not (bx[1] < plo or phi < bx[0] or bx[3] < flo or fhi < bx[2])
                if ov and (excl or aw):
                    deps.add(oi)
                    if excl and bx[0] <= plo and phi <= bx[1] and bx[2] <= flo and fhi <= bx[3]:
                        continue
                newacc.append(a)
            t.acc = newacc
        return deps

    def _drambox(self, v):
        a = v.ap
        off = a.offset
        hi = off
        for st, cn in a.ap:
            if st > 0:
                hi += (cn - 1) * st
        return (0, 0, off, hi)

    def _record(self, idx, reads, writes):
        for (v, w) in [(r, False) for r in reads] + [(x, True) for x in writes]:
            t = v.t
            bx = self._drambox(v) if t.space == "dram" else v.box()
            t.acc.append((idx, w or t.whole, bx[0], bx[1], bx[2], bx[3]))

    def op(self, eng, fn, reads, writes, isdma=False):
        o = Op()
        o.eng, o.fn, o.isdma = eng, fn, isdma
        o.idx = len(self.ops)
        deps = self._deps(reads, writes)
        self._record(o.idx, reads, writes)
        need = {}
        for di in deps:
            d = self.ops[di]
            if d.isdma:
                key = ("d", d.eng, d.dsem)
                val = d.dval
            else:
                if d.eng == eng and not isdma and (eng == "pe" or getattr(self, "nosame", False)):
                    continue
                key = ("e", d.eng)
                val = d.seq
            if need.get(key, 0) < val:
                need[key] = val
        if isdma:
            n = self.ndma[eng]
            self.ndma[eng] = n + 1
            o.dsem = n % NDSEM
            o.dval = 16 * (n // NDSEM + 1)
            if n >= NDSEM:
                key = ("d", eng, o.dsem)
                if need.get(key, 0) < o.dval - 16:
                    need[key] = o.dval - 16
            o.seq = 0
        else:
            self.nseq[eng] += 1
            o.seq = self.nseq[eng]
            o.dsem = o.dval = None
        pb = self.pend.pop(eng, None)
        if pb:
            for key, val in pb.items():
                if need.get(key, 0) < val:
                    need[key] = val
        w = self.waited[eng]
        o.waits = []
        for key, val in need.items():
            if w.get(key, 0) >= val:
                continue
            w[key] = val
            o.waits.append((key, val))
        self.ops.append(o)
        return o

    def _E(self, eng):
        nc = self.nc
        return {"pe": nc.tensor, "dve": nc.vector, "act": nc.scalar, "pool": nc.gpsimd, "sp": nc.sync}[eng]

    def dma(self, out, in_, eng="sp", **kw):
        self.op(eng, lambda e: e.dma_start(out=out.ap, in_=in_.ap, **kw), [in_], [out], isdma=True)

    def mm(self, out, lhsT, rhs, start=True, stop=True, **kw):
        self.op("pe", lambda e: e.matmul(out.ap, lhsT.ap, rhs.ap, start=start, stop=stop, **kw), [lhsT, rhs], [out])

    def tr(self, out, in_, ident):
        self.op("pe", lambda e: e.transpose(out.ap, in_.ap, ident.ap), [in_, ident], [out])

    def act(self, out, in_, func, bias=None, scale=1.0, eng="act", extra_reads=()):
        rd = [in_] + list(extra_reads)
        b = bias
        s = scale
        if isinstance(bias, V):
            rd.append(bias)
        if isinstance(scale, V):
            rd.append(scale)

        def f(e):
            kw = {}
            if b is not None:
                kw["bias"] = b.ap if isinstance(b, V) else b
            kw["scale"] = s.ap if isinstance(s, V) else s
            return e.activation(out.ap, in_.ap, func, **kw)
        self.op(eng, f, rd, [out])

    def tt(self, out, in0, in1, op, eng="dve"):
        self.op(eng, lambda e: e.tensor_tensor(out.ap, in0.ap, in1.ap, op), [in0, in1], [out])

    def ts(self, out, in0, s1, op0, s2=None, op1=None, eng="dve"):
        rd = [in0]
        if isinstance(s1, V):
            rd.append(s1)
        if isinstance(s2, V):
            rd.append(s2)

        def f(e):
            a1 = s1.ap if isinstance(s1, V) else s1
            a2 = s2.ap if isinstance(s2, V) else s2
            if op1 is None:
                return e.tensor_scalar(out.ap, in0.ap, a1, None, op0)
            return e.tensor_scalar(out.ap, in0.ap, a1, a2, op0, op1)
        self.op(eng, f, rd, [out])

    def stt(self, out, in0, s, in1, op0, op1, eng="dve"):
        rd = [in0, in1]
        if isinstance(s, V):
            rd.append(s)
        self.op(eng, lambda e: e.scalar_tensor_tensor(out.ap, in0.ap, s.ap if isinstance(s, V) else s, in1.ap, op0, op1), rd, [out])

    def cp(self, out, in_, eng="dve"):
        if eng == "act":
            self.op(eng, lambda e: e.copy(out.ap, in_.ap), [in_], [out])
        else:
            self.op(eng, lambda e: e.tensor_copy(out.ap, in_.ap), [in_], [out])

    def memset(self, out, val, eng="dve"):
        self.op(eng, lambda e: e.memset(out.ap, val), [], [out])

    def recip(self, out, in_):
        self.op("dve", lambda e: e.reciprocal(out.ap, in_.ap), [in_], [out])

    def finish(self, out_tiles=()):
        nc = self.nc
        final_waits = []
        for e in ENGS:
            if self.nseq[e] > 0:
                final_waits.append((("e", e), self.nseq[e]))
            n = self.ndma[e]
            for k in range(min(n, NDSEM)):
                cnt = (n - 1 - k) // NDSEM + 1
                final_waits.append((("d", e, k), 16 * cnt))
        es = self.es
        sems = {}
        for e in ENGS:
            sems[("e", e)] = es.enter_context(nc.semaphore("s_" + e))
            for k in range(NDSEM):
                sems[("d", e, k)] = es.enter_context(nc.semaphore("d_%s_%d" % (e, k)))
        per = {e: [] for e in ENGS}
        for o in self.ops:
            per[o.eng].append(o)
        blk = es.enter_context(nc.Block())

        def mk(engname):
            def body(e):
                for o in per[engname]:
                    for key, val in o.waits:
                        e.wait_ge(sems[key], val)
                    ins = o.fn(e)
                    if o.isdma:
                        ins.then_inc(sems[("d", o.eng, o.dsem)], 16)
                    else:
                        ins.then_inc(sems[("e", o.eng)], 1)
                if engname == "sp":
                    for key, val in final_waits:
                        e.wait_ge(sems[key], val)
            return body
        blk.tensor(mk("pe"))
        blk.vector(mk("dve"))
        blk.scalar(mk("act"))
        blk.gpsimd(mk("pool"))
        blk.sync(mk("sp"))
        es.close()
        return nc


D = 1024
DIN = 8464
DFF = 4096
EPS = 1e-6
NCF = 1184
NCB = 768
PI = float(np.pi)


def host_consts():
    import ml_dtypes
    p = np.arange(128)[:, None]
    f = np.arange(128)[None, :]
    same = (p // 64) == (f // 64)
    cf = np.zeros((128, NCF), np.float32)
    cf[:, 0:128] = np.eye(128)
    cf[:, 128:256] = ((p <= f) & same)
    cf[:, 256:384] = (p == 64 * (f // 64) + 63)
    cf[:, 384:512] = (p == 63)
    cf[:, 512:640] = (p == 127)
    cf[:, 640:768] = ((f > p) & same)
    cf[:, 768:896] = ((f >= p) & same)
    cf[:, 896:1024] = ((p > f) & same)
    cf[:, 1024:1152] = 1.0
    invf = (500000.0 ** (-np.arange(0, 16, 2, dtype=np.float32) / 16)).astype(np.float32)
    cf[:, 1152:1160] = invf[None, :]
    cf[:, 1160:1168] = invf[None, :]
    cf[:, 1168:1176] = 0.0
    cf[:, 1176:1184] = np.pi / 2
    cb = np.zeros((128, NCB), np.float32)
    cb[:, 0:128] = np.eye(128)
    cb[:, 128:256] = 1.0
    cb[:, 256:384] = (p >= f)
    cb[:, 384:512] = (p <= f)
    cb[:, 512:640] = (p > f)
    cb[:, 640:768] = (p <= f)
    return cf, cb.astype(ml_dtypes.bfloat16)


def build(S, dbg=False, phases=("W", "X", "L0", "L1", "F"), sub=("P1", "P2", "P2b", "P3")):
    k = K()
    NB = S // 128
    NT = S // 512
    EXT_I, EXT_O = "ExternalInput", "ExternalOutput"
    SCR = EXT_O if dbg else "Internal"
    x_in = k.dram("x", [S, D], F32, EXT_I)
    pos_in = k.dram("pos", [S], I32, EXT_I)
    cf_in = k.dram("cf", [128, NCF], F32, EXT_I)
    cb_in = k.dram("cb", [128, NCB], BF16, EXT_I)
    w_in = k.dram("w_in", [2, D, DIN], F32, EXT_I)
    b_in = k.dram("b_in", [2, DIN], F32, EXT_I)
    conv_w = k.dram("conv_w", [2, 4, 2048], F32, EXT_I)
    a_log = k.dram("a_log", [2, 8], F32, EXT_I)
    dt_bias = k.dram("dt_bias", [2, 8], F32, EXT_I)
    sinks = k.dram("sinks", [2, 8], F32, EXT_I)
    c_norm = k.dram("c_norm", [2, 128], F32, EXT_I)
    w_ba = k.dram("w_branch_a", [2, 512, D], F32, EXT_I)
    w_bb = k.dram("w_branch_b", [2, 512, D], F32, EXT_I)
    w_bc = k.dram("w_branch_c", [2, D, D], F32, EXT_I)
    w_out = k.dram("w_out", [2, D, D], F32, EXT_I)
    norm_mix = k.dram("norm_mix", [2, D], F32, EXT_I)
    norm_ffn = k.dram("norm_ffn", [2, D], F32, EXT_I)
    w_ff1 = k.dram("w_ff1", [2, D, DFF], F32, EXT_I)
    w_ff2 = k.dram("w_ff2", [2, DFF, D], F32, EXT_I)
    norm_final = k.dram("norm_final", [D], F32, EXT_I)
    out = k.dram("out", [S, D], F32, EXT_O)
    Wb_in = k.dram("Wb_in", [2, D, DIN], BF16, "Internal")
    Wb_ba = k.dram("Wb_ba", [2, 512, D], BF16, "Internal")
    Wb_bb = k.dram("Wb_bb", [2, 512, D], BF16, "Internal")
    Wb_bc = k.dram("Wb_bc", [2, D, D], BF16, "Internal")
    Wb_out = k.dram("Wb_out", [2, D, D], BF16, "Internal")
    Wb_f1 = k.dram("Wb_f1", [2, D, DFF], BF16, "Internal")
    Wb_f2 = k.dram("Wb_f2", [2, DFF, D], BF16, "Internal")
    xT = [k.dram("xT%d" % i, [D, S], F32, SCR) for i in range(3)]
    hT_d = k.dram("hT_d", [D, S], BF16, SCR)
    qkv_d = k.dram("qkv_d", [S, 2304], BF16, SCR)
    R_d = [k.dram("R%d" % i, [S, 520], F32, SCR) for i in range(4)]
    ycT_d = k.dram("ycT_d", [D, S], BF16, SCR)

    k.arena(198 * 1024)
    PB = [k.ps("PB%d" % i, [128, 512]) for i in range(8)]

    def pbf(i):
        return V(PB[i], PB[i].h[:].bitcast(BF16))

    cf = k.sb("cf", [128, NCF], F32)
    cb = k.sb("cb", [128, NCB], BF16)
    k.dma(cf[:], cf_in[:])
    k.dma(cb[:], cb_in[:])
    identf = cf[:, 0:128]
    tri = cf[:, 128:256]
    sellastblk = cf[:, 256:384]
    sellast = [cf[:, 384:512], cf[:, 512:640]]
    maskU = cf[:, 640:768]
    maskUi = cf[:, 768:896]
    maskL = cf[:, 896:1024]
    onesf = cf[:, 1024:1152]
    invf = cf[:, 1152:1168]
    phase = cf[:, 1168:1184]
    identb = cb[:, 0:128]
    onesb = cb[:, 128:256]
    maskA = cb[:, 256:512]
    maskB = cb[:, 512:768]
    cs = k.sb("cs", [128, NB, 16], F32)
    k.set_mark()

    def bcast_mid(v, n):
        return V(v.t, v.ap.unsqueeze(1).to_broadcast([v.ap.shape[0], n, v.ap.shape[1]]))

    def bcast_last(v, n):
        return V(v.t, v.ap.unsqueeze(2).to_broadcast([v.ap.shape[0], v.ap.shape[1], n]))

    def colvec(dram_v, nchunks, name):
        t = k.sb(name, [128, nchunks], F32)
        k.dma(t[:], V(dram_v.t, dram_v.ap.rearrange("(c p) -> p c", p=128)), allow_slow_non_contiguous=True)
        return t

    def rowbc(dram_v, n, name):
        t = k.sb(name, [128, n], F32)
        k.dma(t[:], V(dram_v.t, dram_v.ap.partition_broadcast(128)))
        return t

    rr = {"i": 0}

    def rot_eng(engs=("dve", "act", "pool")):
        rr["i"] += 1
        return engs[rr["i"] % len(engs)]

    def cast_matrix(src, dst, R, C):
        st32 = [k.sb("st32_%d" % i, [128, 2048], F32) for i in range(3)]
        st16 = [k.sb("st16_%d" % i, [128, 2048], BF16) for i in range(3)]
        i = 0
        for r0 in range(0, R, 128):
            for c0 in range(0, C, 2048):
                cw = min(2048, C - c0)
                a, b = st32[i % 3], st16[i % 3]
                k.dma(a[:, 0:cw], src[r0:r0 + 128, c0:c0 + cw])
                k.cp(b[:, 0:cw], a[:, 0:cw], eng=rot_eng())
                k.dma(dst[r0:r0 + 128, c0:c0 + cw], b[:, 0:cw])
                i += 1

    if "W" in phases:
        k.phase_reset()
        st32 = [k.sb("st32_%d" % i, [128, 2048], F32) for i in range(3)]
        st16 = [k.sb("st16_%d" % i, [128, 2048], BF16) for i in range(3)]
        cnt = [0]

        def castm(src, dst, R, C):
            for r0 in range(0, R, 128):
                for c0 in range(0, C, 2048):
                    cw = min(2048, C - c0)
                    a, b = st32[cnt[0] % 3], st16[cnt[0] % 3]
                    k.dma(a[:, 0:cw], src[r0:r0 + 128, c0:c0 + cw])
                    k.cp(b[:, 0:cw], a[:, 0:cw], eng=rot_eng())
                    k.dma(dst[r0:r0 + 128, c0:c0 + cw], b[:, 0:cw])
                    cnt[0] += 1
        for l in range(2):
            castm(w_in[l], Wb_in[l], D, DIN)
            castm(w_ba[l], Wb_ba[l], 512, D)
            castm(w_bb[l], Wb_bb[l], 512, D)
            castm(w_bc[l], Wb_bc[l], D, D)
            castm(w_out[l], Wb_out[l], D, D)
            castm(w_ff1[l], Wb_f1[l], D, DFF)
            castm(w_ff2[l], Wb_f2[l], DFF, D)

    def fm(dram_t):
        return V(dram_t, dram_t.h.ap().rearrange("(c p) s -> p c s", p=128))

    if "X" in phases:
        k.phase_reset()
        xtok = [k.sb("xtok%d" % i, [128, D], F32) for i in range(2)]
        xTt = [k.sb("xTt%d" % i, [128, 8, 512], F32) for i in range(2)]
        for blk in range(NB):
            a = xtok[blk % 2]
            k.dma(a[:], x_in[blk * 128:(blk + 1) * 128, :])
            tt_ = xTt[(blk // 4) % 2]
            for c in range(8):
                k.tr(PB[c // 4][:, (c % 4) * 128:(c % 4) * 128 + 128], a[:, c * 128:(c + 1) * 128], identf)
            j = blk % 4
            k.cp(tt_[:, 0:4, j * 128:(j + 1) * 128], PB[0][:, :].re("p (a b) -> p a b", a=4), eng="dve")
            k.cp(tt_[:, 4:8, j * 128:(j + 1) * 128], PB[1][:, :].re("p (a b) -> p a b", a=4), eng="act")
            if j == 3:
                t = blk // 4
                k.dma(fm(xT[0])[:, :, t * 512:(t + 1) * 512], tt_[:])
        posi = k.sb("posi", [128, NB], I32)
        posf = k.sb("posf", [128, NB], F32)
        ang = k.sb("ang", [128, NB, 16], F32)
        ki = k.sb("ki", [128, NB, 16], I32)
        kf = k.sb("kf", [128, NB, 16], F32)
        msk = k.sb("msk", [128, NB, 16], F32)
        k.dma(posi[:], V(pos_in, pos_in.h.ap().rearrange("(n p) -> p n", p=128)), allow_slow_non_contiguous=True)
        k.cp(posf[:], posi[:])
        k.tt(ang[:], bcast_last(posf[:], 16), bcast_mid(invf, NB), ALU.mult)
        k.tt(ang[:], ang[:], bcast_mid(phase, NB), ALU.add)
        k.ts(kf[:], ang[:], 1.0 / (2 * PI), ALU.mult)
        k.cp(ki[:], kf[:])
        k.cp(kf[:], ki[:])
        k.stt(ang[:], kf[:], -2 * PI, ang[:], ALU.mult, ALU.add)
        k.ts(msk[:], ang[:], PI, ALU.is_gt)
        k.stt(ang[:], msk[:], -2 * PI, ang[:], ALU.mult, ALU.add)
        k.ts(msk[:], ang[:], -PI, ALU.is_lt)
        k.stt(ang[:], msk[:], 2 * PI, ang[:], ALU.mult, ALU.add)
        k.ts(ang[:], ang[:], PI, ALU.min, -PI, ALU.max)
        k.act(cs[:], ang[:], AF.Sin)
        if dbg:
            cs_d = k.dram("cs_d", [128, NB * 16], F32, EXT_O)
            k.dma(cs_d[:], cs[:, :, :].re("p a b -> p (a b)"))

    def rmsnorm_T(xt, gain, hT, sq, rstd, pbank):
        k.act(sq[:, 0:4, :], xt[:, 0:4, :], AF.Square)
        k.act(sq[:, 4:8, :], xt[:, 4:8, :], AF.Square)
        for c in range(8):
            k.mm(pbank[:, :], onesb, sq[:, c, :], start=(c == 0), stop=(c == 7))
        k.act(rstd[:], pbank[:, :], AF.Sqrt, bias=EPS, scale=1.0 / D)
        k.recip(rstd[:], rstd[:])
        for c in range(8):
            k.stt(hT[:, c, :], xt[:, c, :], gain[:, c:c + 1], rstd[:], ALU.mult, ALU.mult)

    def layer(l, xsrc, xdst):
        if "P1" in sub:
            proj1(l, xsrc)
        if "P2" in sub:
            attn(l)
        if "P2b" in sub:
            k.phase_reset()
            deltanet(l)
        if "P3" in sub:
            k.phase_reset()
            post(l, xsrc, xdst)

    def proj1(l, xsrc):
        k.phase_reset()
        W_AB = k.sb("W_AB", [128, 8, 2304], BF16)
        k.dma(W_AB[:], V(Wb_in, Wb_in.h.ap()[l].rearrange("(c p) n -> p c n", p=128)[:, :, 0:2304]))
        biasAB = rowbc(b_in[l:l + 1, 0:2304], 2304, "biasAB")
        gain = colvec(norm_mix[l], 8, "gain")
        xt_ = [k.sb("xt%d" % i, [128, 8, 512], F32) for i in range(2)]
        sq = k.sb("sq", [128, 8, 512], BF16)
        rstd = k.sb("rstd", [128, 512], F32)
        hT_ = [k.sb("hT%d" % i, [128, 8, 512], BF16) for i in range(2)]
        u = k.sb("u", [128, 2304], F32)
        ub_ = [k.sb("ub%d" % i, [128, 2304], BF16) for i in range(2)]
        rt = [k.sb("rt%d" % i, [128, 16, 8], F32) for i in range(4)]
        for t in range(NT):
            xt, hT = xt_[t % 2], hT_[t % 2]
            k.dma(xt[:], fm(xsrc)[:, :, t * 512:(t + 1) * 512])
            rmsnorm_T(xt, gain, hT, sq, rstd, PB[7])
            k.dma(fm(hT_d)[:, :, t * 512:(t + 1) * 512], hT[:])
            for j in range(4):
                blk = t * 4 + j
                ub = ub_[blk % 2]
                for cg in range(5):
                    w = 512 if cg < 4 else 256
                    pb = PB[cg % 4]
                    for c in range(8):
                        k.mm(pb[:, 0:w], hT[:, c, j * 128:(j + 1) * 128], W_AB[:, c, cg * 512:cg * 512 + w],
                             start=(c == 0), stop=(c == 7))
                    k.tt(u[:, cg * 512:cg * 512 + w], pb[:, 0:w], biasAB[:, cg * 512:cg * 512 + w], ALU.add)
                k.cp(ub[:], u[:], eng="act")
                uv = u[:, :].re("p (h d) -> p h d", d=64)
                ubv = ub[:, :].re("p (h d) -> p h d", d=64)
                for (h0, nh) in ((0, 16), (24, 10)):
                    x1 = uv[:, h0:h0 + nh, 0:8]
                    x2 = uv[:, h0:h0 + nh, 8:16]
                    cosb = bcast_mid(cs[:, blk, 8:16], nh)
                    sinb = bcast_mid(cs[:, blk, 0:8], nh)
                    t1, t2, t3, t4 = (r_[:, 0:nh, :] for r_ in rt)
                    k.tt(t1, x1, cosb, ALU.mult)
                    k.tt(t2, x2, sinb, ALU.mult, eng="pool")
                    k.tt(t3, x2, cosb, ALU.mult)
                    k.tt(t4, x1, sinb, ALU.mult, eng="pool")
                    k.tt(ubv[:, h0:h0 + nh, 0:8], t1, t2, ALU.subtract)
                    k.tt(ubv[:, h0:h0 + nh, 8:16], t3, t4, ALU.add, eng="pool")
                k.dma(qkv_d[blk * 128:(blk + 1) * 128, :], ub[:])

    def attn(l):
        k.phase_reset()
        NQ = 3
        Qb_ = [k.sb("Qb%d" % i, [128, 512], BF16) for i in range(NQ)]
        Kb_ = [k.sb("Kb%d" % i, [128, 512], BF16) for i in range(NQ)]
        Vb_ = [k.sb("Vb%d" % i, [128, 8, 72], BF16) for i in range(4)]
        QT_ = [k.sb("QT%d" % i, [128, 2, 4, 128], BF16) for i in range(NQ)]
        for qt in QT_:
            k.memset(qt[64:128, 0, :, :], 0.0)
            k.memset(qt[0:64, 1, :, :], 0.0, eng="pool")
        KT_ = [k.sb("KT%d" % i, [128, 4, 128], BF16) for i in range(4)]
        PT_ = [[k.sb("PT%d_%d" % (s_, i), [128, 2, 2, 128], BF16) for i in range(4)] for s_ in range(2)]
        O_ = [k.sb("O%d" % i, [128, 8, 65], F32) for i in range(2)]
        for vb in Vb_:
            k.memset(vb[:, :, 64:65], 1.0)
        groups = []
        for (ci, d, qoff, koff, voff, nkv, isB) in ((0, 1, 0, 512, 1024, 8, False), (1, 4, 0, 512, 1024, 8, False),
                                                     (2, 16, 0, 512, 1024, 8, False), (3, 1, 1536, 2048, 2176, 2, True)):
            L = S // d
            for r in range(d):
                for n in range(L // 128):
                    groups.append((ci, d, qoff, koff, voff, nkv, isB, r, n))
        G = len(groups)

        def rows_of(g):
            (ci, d, qoff, koff, voff, nkv, isB, r, n) = groups[g]
            lo = r + d * 128 * n
            hi = lo + d * 127 + 1
            return lo, hi

        def stage1(g):
            (ci, d, qoff, koff, voff, nkv, isB, r, n) = groups[g]
            nkc = 4 if not isB else 1
            lo, hi = rows_of(g)
            rows = qkv_d.h.ap()[lo:hi:d, :]
            Qb, Kb, Vb = Qb_[g % NQ], Kb_[g % NQ], Vb_[g % 4]
            QT, KT = QT_[g % NQ], KT_[g % 4]
            if not isB:
                k.dma(Qb[:], V(qkv_d, rows[:, qoff:qoff + 512]))
            else:
                for g_ in range(2):
                    k.dma(Qb[:, :].re("p (j g e) -> p j g e", j=4, g=2)[:, :, g_, :],
                          V(qkv_d, rows[:, qoff + g_ * 256:qoff + g_ * 256 + 256].rearrange("t (j e) -> t j e", j=4)))
            k.dma(Kb[:, 0:nkc * 128], V(qkv_d, rows[:, koff:koff + nkc * 128]))
            k.dma(Vb[:, 0:nkv, 0:64], V(qkv_d, rows[:, voff:voff + nkv * 64].rearrange("t (h e) -> t h e", e=64)))
            pq, pk = pbf(6), pbf(7)
            for c in range(4):
                k.tr(pq[:, c * 128:(c + 1) * 128], Qb[:, c * 128:(c + 1) * 128], identb)
            k.cp(QT[0:64, 0, :, :], pq[0:64, 0:512].re("p (a b) -> p a b", a=4), eng="dve")
            k.cp(QT[64:128, 1, :, :], pq[64:128, 0:512].re("p (a b) -> p a b", a=4), eng="dve")
            for c in range(nkc):
                k.tr(pk[:, c * 128:(c + 1) * 128], Kb[:, c * 128:(c + 1) * 128], identb)
            k.cp(KT[:, 0:nkc, :], pk[:, 0:nkc * 128].re("p (a b) -> p a b", a=nkc), eng="act")

        def stage2(g):
            (ci, d, qoff, koff, voff, nkv, isB, r, n) = groups[g]
            mask = maskB if isB else maskA
            mask4 = V(mask.t, mask.ap.rearrange("p (a b) -> p a b", a=2).unsqueeze(1).to_broadcast([128, 2, 2, 128]))
            QT, KT, KTp = QT_[g % NQ], KT_[g % 4], KT_[(g - 1) % 4]
            kcs = (0, 1) if n > 0 else (1,)
            ksl = slice(0, 2) if n > 0 else slice(1, 2)
            for half in range(2):
                for bq in range(2):
                    pb = PB[half * 2 + bq]
                    PT = PT_[g % 2][half * 2 + bq]
                    for hh in range(2):
                        h = half * 4 + bq + 2 * hh
                        for kc in kcs:
                            src = KTp if kc == 0 else KT
                            kch = (h // 2) if not isB else 0
                            col = (hh * 2 + kc) * 128
                            k.mm(pb[:, col:col + 128], src[:, kch, :], QT[:, h % 2, h // 2, :])
                    k.act(PT[:, 0:2, ksl, :], pb[:, :].re("p (a b c) -> p a b c", a=2, b=2)[:, :, ksl, :], AF.Exp, scale=0.125)
                    k.tt(PT[:, 0:2, ksl, :], PT[:, 0:2, ksl, :], mask4[:, :, ksl, :], ALU.mult, eng=("dve" if bq == 0 else "pool"))

        def stage3(g):
            (ci, d, qoff, koff, voff, nkv, isB, r, n) = groups[g]
            lo, hi = rows_of(g)
            Vb, Vp = Vb_[g % 4], Vb_[(g - 1) % 4]
            O = O_[g % 2]
            kcs = (0, 1) if n > 0 else (1,)
            for half in range(2):
                po = PB[4 + half]
                for bq in range(2):
                    PT = PT_[g % 2][half * 2 + bq]
                    for hh in range(2):
                        h = half * 4 + bq + 2 * hh
                        hs = h - half * 4
                        kvh = h if not isB else (h % 2)
                        for kc in kcs:
                            vsrc = Vp if kc == 0 else Vb
                            k.mm(po[:, hs * 65:hs * 65 + 65], PT[:, hh, kc, :], vsrc[:, kvh, 0:65],
                                 start=(kc == kcs[0]), stop=(kc == 1))
                k.cp(O[:, half * 4:half * 4 + 4, :], po[:, 0:260].re("p (a b) -> p a b", a=4),
                     eng=("dve" if half == 0 else "act"))
            k.dma(V(R_d[ci], R_d[ci].h.ap()[lo:hi:d, :]), O[:, :, :].re("p a b -> p (a b)"))

        for s_ in range(G + 2):
            if s_ < G:
                stage1(s_)
            if 0 <= s_ - 1 < G:
                stage2(s_ - 1)
            if 0 <= s_ - 2 < G:
                stage3(s_ - 2)

    def deltanet(l):
        W_C = k.sb("W_C", [128, 8, 3088], BF16)
        k.dma(W_C[:], V(Wb_in, Wb_in.h.ap()[l].rearrange("(c p) n -> p c n", p=128)[:, :, 2304:5392]))
        biasC = colvec(b_in[l, 2304:5376], 24, "biasC")
        biasab = rowbc(b_in[l:l + 1, 5376:5392], 16, "biasab")
        convw = k.sb("convw", [128, 16, 4], F32)
        for j_ in range(4):
            k.dma(convw[:, :, j_], V(conv_w, conv_w.h.ap()[l, j_].rearrange("(c p) -> p c", p=128)), allow_slow_non_contiguous=True)
        alog = rowbc(a_log[l:l + 1, :], 8, "alog")
        dtb = rowbc(dt_bias[l:l + 1, :], 8, "dtb")
        cnorm = colvec(c_norm[l], 1, "cnorm")
        nalog = k.sb("nalog", [128, 8], F32)
        k.act(nalog[:], alog[:], AF.Exp)
        k.ts(nalog[:], nalog[:], -1.0, ALU.mult)
        hT_ = [k.sb("dhT%d" % i, [128, 8, 512], BF16) for i in range(1)]
        U_ = [k.sb("U%d" % i, [128, 515], F32) for i in range(1)]
        carry = k.sb("carry", [128, 16, 3], F32)
        k.memset(carry[:], 0.0)
        ytmp = [k.sb("ytmp%d" % i, [128, 512], F32) for i in range(1)]
        cvo = k.sb("cvo", [128, 512], F32)
        sqc = k.sb("sqc", [128, 512], BF16)
        rs = k.sb("rs", [128, 512], F32)
        qkT = k.sb("qkT", [128, 8, 512], BF16)
        vT = k.sb("vT", [128, 8, 512], BF16)
        zs1 = k.sb("zs1", [128, 512], BF16)
        ab = k.sb("ab", [128, 4, 16], F32)
        xa = k.sb("xa", [128, 4, 8], F32)
        t8a = k.sb("t8a", [128, 4, 8], F32)
        t8b = k.sb("t8b", [128, 4, 8], F32)
        g_ = k.sb("g_", [128, 4, 8], F32)
        beta_ = k.sb("beta_", [128, 4, 8], F32)
        sc = k.sb("sc", [128, 16], F32)
        gcl = k.sb("gcl", [128, 8], F32)
        kd = k.sb("kd", [128, 8], F32)
        eg = k.sb("eg", [128, 8], F32)
        bk = k.sb("bk", [128, 8], F32)
        gl2 = [[k.sb("gl%d_%d" % (s_, c), [128, 8], F32) for c in range(2)] for s_ in range(2)]
        Dm = k.sb("Dm", [128, 8, 128], F32)
        diff1 = k.sb("diff1", [128, 8, 128], F32)
        tA = Dm
        dT = k.sb("dT", [128, 8, 128], BF16)
        d2 = k.sb("d2", [128, 8, 128], BF16)
        Re2 = [k.sb("Re%d" % i, [128, 8, 128], F32) for i in range(2)]
        Rb = k.sb("Rb", [128, 8, 128], BF16)
        kkt_s = k.sb("kkt_s", [128, 4, 128], BF16)
        qk_s = k.sb("qk_s", [128, 4, 128], BF16)
        Xb = [k.sb("Xb%d" % i, [128, 8, 128], BF16) for i in range(2)]
        Yb = [k.sb("Yb%d" % i, [128, 8, 128], BF16) for i in range(2)]
        Qb2 = [k.sb("Qq%d" % i, [128, 8, 128], BF16) for i in range(2)]
        attnT2 = [k.sb("attnT%d" % i, [128, 8, 128], BF16) for i in range(2)]
        TT2 = [k.sb("TT%d" % i, [128, 8, 128], BF16) for i in range(2)]
        ktok = k.sb("ktok", [128, 4, 128], BF16)
        vtok = k.sb("vtok", [128, 8, 128], BF16)
        kb_ = k.sb("kb_", [128, 8, 128], BF16)
        vb2 = [k.sb("vb_%d" % i, [128, 8, 128], BF16) for i in range(2)]
        kdec2 = [k.sb("kdec%d" % i, [128, 8, 128], BF16) for i in range(2)]
        nwT2 = [k.sb("nwT%d" % i, [128, 8, 128], BF16) for i in range(2)]
        vn = k.sb("vn", [128, 8, 128], BF16)
        Sf = k.sb("Sf", [128, 8, 128], F32)
        Sb = k.sb("Sb", [128, 8, 128], BF16)
        t1 = k.sb("t1", [128, 8, 64], F32)
        oT = k.sb("oT", [128, 8, 512], F32)
        ycT_ = [k.sb("ycT%d" % i, [128, 512], BF16) for i in range(2)]
        k.memset(Sf[:], 0.0)
        k.memset(Sb[:], 0.0)
        k.memset(vn[:], 0.0)

        def v4(t):
            return t[:, :, :].re("p (a b) c -> p a b c", a=4)

        def b4(t):
            return V(t, t[:, :, :].ap.unsqueeze(2).to_broadcast([128, 4, 2, 128]))

        def pb2(i0):
            return PB[i0], PB[i0 + 1]

        for t in range(NT):
            hT = hT_[0]
            k.dma(hT[:], fm(hT_d)[:, :, t * 512:(t + 1) * 512])
            for cc in range(16):
                pb = PB[cc % 4]
                U = U_[0]
                for c in range(8):
                    k.mm(pb[:, :], W_C[:, c, cc * 128:(cc + 1) * 128], hT[:, c, :], start=(c == 0), stop=(c == 7))
                k.cp(U[:, 0:3], carry[:, cc, :], eng="pool")
                k.act(U[:, 3:515], pb[:, :], AF.Identity, bias=biasC[:, cc:cc + 1])
                y = ytmp[0]
                k.ts(y[:], U[:, 3:515], convw[:, cc, 3:4], ALU.mult)
                k.stt(y[:], U[:, 2:514], convw[:, cc, 2:3], y[:], ALU.mult, ALU.add)
                k.stt(y[:], U[:, 1:513], convw[:, cc, 1:2], y[:], ALU.mult, ALU.add)
                k.stt(y[:], U[:, 0:512], convw[:, cc, 0:1], y[:], ALU.mult, ALU.add)
                k.cp(carry[:, cc, :], U[:, 512:515], eng="pool")
                if cc < 8:
                    k.act(cvo[:], y[:], AF.Silu)
                    k.act(sqc[:], cvo[:], AF.Square)
                    k.mm(PB[4 + cc % 2][:, :], onesb, sqc[:])
                    k.act(rs[:], PB[4 + cc % 2][:, :], AF.Sqrt, bias=EPS, scale=1.0)
                    k.recip(rs[:], rs[:])
                    k.stt(qkT[:, cc, :], cvo[:], (128.0 ** -0.5) if cc < 4 else 1.0, rs[:], ALU.mult, ALU.mult)
                else:
                    k.act(vT[:, cc - 8, :], y[:], AF.Silu)
            for j in range(4):
                for c in range(8):
                    k.mm(PB[6][:, j * 16:(j + 1) * 16], hT[:, c, j * 128:(j + 1) * 128], W_C[:, c, 3072:3088],
                         start=(c == 0), stop=(c == 7))
            k.tt(ab[:], PB[6][:, 0:64].re("p (a b) -> p a b", a=4), bcast_mid(biasab[:, :], 4), ALU.add)
            k.tt(xa[:], ab[:, :, 0:8], bcast_mid(dtb[:, :], 4), ALU.add)
            k.ts(t8a[:], xa[:], -1.0, ALU.mult)
            k.tt(t8a[:], t8a[:], xa[:], ALU.max)
            k.act(t8a[:], t8a[:], AF.Exp, scale=-1.0)
            k.act(t8a[:], t8a[:], AF.Ln, bias=1.0)
            k.stt(t8b[:], xa[:], 0.0, t8a[:], ALU.max, ALU.add)
            k.tt(g_[:], t8b[:], bcast_mid(nalog[:, :], 4), ALU.mult)
            k.act(beta_[:], ab[:, :, 8:16], AF.Sigmoid)
            def pre(j, t=t):
                bc0 = j * 128
                cols = slice(bc0, bc0 + 128)
                gblk = t * 4 + j
                sidx = gblk % 2
                gl, Re, attnT, TT = gl2[sidx], Re2[sidx], attnT2[sidx], TT2[sidx]
                vb_, kdec, nwT = vb2[sidx], kdec2[sidx], nwT2[sidx]
                gb = g_[:, j, :]
                betab = beta_[:, j, :]
                pk_, pv_ = pbf(0), pbf(1)
                for hq in range(4):
                    k.tr(pk_[:, hq * 128:(hq + 1) * 128], qkT[:, 4 + hq, cols], identb)
                k.cp(ktok[:], pk_[:, 0:512].re("p (a b) -> p a b", a=4))
                for h in range(8):
                    k.tr(pv_[:, h * 128:(h + 1) * 128], vT[:, h, cols], identb)
                k.cp(vtok[:], pv_[:, 0:1024].re("p (a b) -> p a b", a=8), eng="act")
                yield
                k.mm(PB[2][:, 0:8], tri, gb)
                k.cp(sc[:, 0:8], PB[2][:, 0:8])
                k.cp(sc[:, 8:16], betab, eng="pool")
                gc = sc[:, 0:8]
                k.mm(PB[2][:, 8:16], sellastblk, gc)
                k.mm(PB[2][:, 16:24], sellast[0], gc)
                k.mm(PB[2][:, 24:32], sellast[1], gc)
                k.tt(gcl[:], PB[2][:, 8:16], gc, ALU.subtract)
                k.act(kd[:], gcl[:], AF.Exp)
                k.act(gl[0][:], PB[2][:, 16:24], AF.Exp)
                k.act(gl[1][:], PB[2][:, 24:32], AF.Exp)
                k.act(eg[:], gc, AF.Exp)
                k.tt(bk[:], betab, eg[:], ALU.mult)
                k.tt(v4(kb_), b4(ktok), bcast_last(bk[:, :], 128).re("p (a b) c -> p a b c", a=4), ALU.mult)
                k.tt(vb_[:], vtok[:], bcast_last(sc[:, 8:16], 128), ALU.mult, eng="pool")
                k.tt(v4(kdec), b4(ktok), bcast_last(kd[:, :], 128).re("p (a b) c -> p a b c", a=4), ALU.mult)
                yield
                k.tt(Dm[:], bcast_mid(identf, 8), bcast_last(sc[:, 0:8], 128), ALU.mult)
                for q in range(2):
                    k.mm(PB[q][:, :], onesf, Dm[:, 4 * q:4 * q + 4, :].re("p a b -> p (a b)"))
                k.tt(Dm[:], bcast_mid(identf, 8), bcast_last(sc[:, 8:16], 128), ALU.mult)
                for q in range(2):
                    k.mm(PB[2 + q][:, :], onesf, Dm[:, 4 * q:4 * q + 4, :].re("p a b -> p (a b)"))
                for hb in range(2):
                    k.tt(diff1[:, 4 * hb:4 * hb + 4, :], PB[hb][:, :].re("p (a b) -> p a b", a=4),
                         bcast_last(sc[:, 4 * hb:4 * hb + 4], 128), ALU.subtract)
                    k.act(Re[:, 4 * hb:4 * hb + 4, :], PB[hb][:, :].re("p (a b) -> p a b", a=4), AF.Exp)
                    k.cp(Rb[:, 4 * hb:4 * hb + 4, :], PB[2 + hb][:, :].re("p (a b) -> p a b", a=4), eng="act")
                yield
                k.ts(tA[:], diff1[:], 0.0, ALU.min)
                k.act(dT[:], tA[:], AF.Exp)
                k.ts(tA[:], diff1[:], -1.0, ALU.mult, 0.0, ALU.min)
                k.act(d2[:], tA[:], AF.Exp)
                for hq in range(4):
                    k.mm(PB[0][:, hq * 128:(hq + 1) * 128], qkT[:, 4 + hq, cols], qkT[:, 4 + hq, cols])
                for hq in range(4):
                    k.mm(PB[1][:, hq * 128:(hq + 1) * 128], qkT[:, 4 + hq, cols], qkT[:, hq, cols])
                k.cp(kkt_s[:], PB[0][:, :].re("p (a b) -> p a b", a=4))
                k.cp(qk_s[:], PB[1][:, :].re("p (a b) -> p a b", a=4), eng="act")
                yield
                X0, Y0, Q0 = Xb[0], Yb[0], Qb2[0]
                k.tt(tA[:], dT[:], bcast_mid(maskU, 8), ALU.mult)
                k.tt(tA[:], tA[:], Rb[:], ALU.mult)
                k.tt(v4(tA), v4(tA), b4(kkt_s), ALU.mult)
                k.ts(X0[:], tA[:], -1.0, ALU.mult)
                yield
                k.tt(tA[:], dT[:], bcast_mid(maskUi, 8), ALU.mult)
                k.tt(v4(attnT), v4(tA), b4(qk_s), ALU.mult)
                yield
                k.tt(tA[:], d2[:], bcast_mid(maskL, 8), ALU.mult)
                k.tt(tA[:], tA[:], bcast_last(sc[:, 8:16], 128), ALU.mult)
                k.tt(v4(tA), v4(tA), b4(kkt_s), ALU.mult)
                k.ts(Y0[:], tA[:], -1.0, ALU.mult)
                k.tt(Q0[:], X0[:], bcast_mid(identb, 8), ALU.add)
                yield
                Xc, Yc, Qc = X0, Y0, Q0
                for kk in range(1, 6):
                    Xn, Yn = Xb[kk % 2], Yb[kk % 2]
                    Qn = Qb2[kk % 2] if kk < 5 else TT
                    for hb in range(2):
                        hs = range(4 * hb, 4 * hb + 4)
                        by, bx = PB[2 * hb], PB[2 * hb + 1]
                        for h in hs:
                            k.mm(by[:, (h % 4) * 128:(h % 4) * 128 + 128], Xc[:, h, :], Yc[:, h, :])
                        if kk < 5:
                            for h in hs:
                                k.mm(bx[:, (h % 4) * 128:(h % 4) * 128 + 128], Yc[:, h, :], Xc[:, h, :])
                        k.cp(Yn[:, 4 * hb:4 * hb + 4, :], by[:, :].re("p (a b) -> p a b", a=4), eng=("dve" if hb == 0 else "act"))
                        if kk < 5:
                            k.cp(Xn[:, 4 * hb:4 * hb + 4, :], bx[:, :].re("p (a b) -> p a b", a=4), eng=("act" if hb == 0 else "dve"))
                        for h in hs:
                            o_ = by[:, (h % 4) * 128:(h % 4) * 128 + 128]
                            k.mm(o_, identb, Qc[:, h, :], start=True, stop=False)
                            k.mm(o_, Yn[:, h, :], Qc[:, h, :], start=False, stop=True)
                        k.cp(Qn[:, 4 * hb:4 * hb + 4, :], by[:, :].re("p (a b) -> p a b", a=4), eng=("dve" if hb == 0 else "act"))
                        yield
                    Xc, Yc, Qc = Xn, Yn, Qn
                for h in range(8):
                    k.mm(PB[h // 4][:, (h % 4) * 128:(h % 4) * 128 + 128], kb_[:, h, :], TT[:, h, :])
                for hb in range(2):
                    k.ts(nwT[:, 4 * hb:4 * hb + 4, :], PB[hb][:, :].re("p (a b) -> p a b", a=4), -1.0, ALU.mult)
                yield

            def rec(j, t=t):
                bc0 = j * 128
                gblk = t * 4 + j
                sidx = gblk % 2
                gl, Re, attnT, TT = gl2[sidx], Re2[sidx], attnT2[sidx], TT2[sidx]
                vb_, kdec, nwT = vb2[sidx], kdec2[sidx], nwT2[sidx]
                for c in range(2):
                    rsl = slice(64 * c, 64 * c + 64)
                    for h in range(8):
                        o_ = PB[4 + h // 4][:, (h % 4) * 128:(h % 4) * 128 + 128]
                        k.mm(o_, TT[:, h, :], vb_[:, h, :], start=True, stop=False)
                        k.mm(o_, nwT[:, h, :], Sb[:, h, :], start=False, stop=True)
                    for hb in range(2):
                        k.cp(vn[rsl, 4 * hb:4 * hb + 4, :], PB[4 + hb][rsl, :].re("p (a b) -> p a b", a=4),
                             eng=("dve" if hb == 0 else "act"))
                    yield
                    for h in range(8):
                        k.mm(PB[6][:, h * 64:(h + 1) * 64], vn[rsl, h, :], attnT[rsl, h, 64 * c:64 * c + 64])
                    for h in range(8):
                        k.mm(PB[7][:, h * 64:(h + 1) * 64], Sb[:, h, :], qkT[:, h // 2, bc0 + 64 * c:bc0 + 64 * c + 64])
                    k.tt(t1[:], PB[7][:, :].re("p (a b) -> p a b", a=8), Re[:, :, 64 * c:64 * c + 64], ALU.mult)
                    k.tt(oT[:, :, bc0 + 64 * c:bc0 + 64 * c + 64], t1[:], PB[6][:, :].re("p (a b) -> p a b", a=8), ALU.add)
                    yield
                    for h in range(8):
                        k.mm(PB[4 + h // 4][:, (h % 4) * 128:(h % 4) * 128 + 128], kdec[rsl, h, :], vn[rsl, h, :])
                    k.tt(Sf[:], Sf[:], bcast_last(gl[c][:, :], 128), ALU.mult)
                    for hb in range(2):
                        k.tt(Sf[:, 4 * hb:4 * hb + 4, :], Sf[:, 4 * hb:4 * hb + 4, :],
                             PB[4 + hb][:, :].re("p (a b) -> p a b", a=4), ALU.add)
                    k.cp(Sb[:], Sf[:], eng="act")
                    yield

            def run_all(gen):
                for _ in gen:
                    pass

            def interleave(ga, gb, ratio=3):
                a_done = b_done = False
                while not (a_done and b_done):
                    if not a_done:
                        try:
                            next(ga)
                        except StopIteration:
                            a_done = True
                    for _ in range(ratio):
                        if not b_done:
                            try:
                                next(gb)
                            except StopIteration:
                                b_done = True

            run_all(pre(0))
            for j in range(4):
                if j < 3:
                    interleave(rec(j), pre(j + 1))
                else:
                    run_all(rec(j))
            for h in range(8):
                pb = PB[h % 4]
                k.act(sqc[:], oT[:, h, :], AF.Square)
                k.mm(pb[:, :], onesb, sqc[:])
                k.act(rs[:], pb[:, :], AF.Sqrt, bias=EPS, scale=1.0 / 128)
                k.recip(rs[:], rs[:])
                k.stt(cvo[:], oT[:, h, :], cnorm[:, 0:1], rs[:], ALU.mult, ALU.mult)
                pz = PB[4 + h % 4]
                cc = 16 + h
                for c in range(8):
                    k.mm(pz[:, :], W_C[:, c, cc * 128:(cc + 1) * 128], hT[:, c, :], start=(c == 0), stop=(c == 7))
                k.act(zs1[:], pz[:, :], AF.Silu, bias=biasC[:, cc:cc + 1])
                yo = ycT_[h % 2]
                k.tt(yo[:], cvo[:], zs1[:], ALU.mult)
                k.dma(ycT_d[h * 128:(h + 1) * 128, t * 512:(t + 1) * 512], yo[:])

    def post(l, xsrc, xdst):
        sink_e = rowbc(sinks[l:l + 1, :], 8, "sink_e")
        k.act(sink_e[:], sink_e[:], AF.Exp)
        gain2 = colvec(norm_ffn[l], 8, "gain2")
        biasG = colvec(b_in[l, 5392:8464], 24, "biasG")
        NWB = 6
        wb = [k.sb("wb%d" % i, [128, 8, 512], BF16) for i in range(NWB)]
        wcnt = [0]

        def wpanel_k8(dram2d, c0, ncol):
            w = wb[wcnt[0] % NWB]
            wcnt[0] += 1
            k.dma(w[:, :, 0:ncol], V(dram2d.t, dram2d.ap.rearrange("(c p) n -> p c n", p=128)[:, :, c0:c0 + ncol]))
            return w

        def wpanel_rows(dram2d, r0, nchunk, c0, ncol):
            w = wb[wcnt[0] % NWB]
            wcnt[0] += 1
            wv = w[:, :, :].re("p a b -> p (a b)")[:, 0:nchunk * ncol].re("p (a b) -> p a b", a=nchunk)
            k.dma(wv, V(dram2d.t, dram2d.ap[r0:r0 + nchunk * 128, :].rearrange("(c p) n -> p c n", p=128)[:, :, c0:c0 + ncol]))
            return wv

        xt_ = [k.sb("pxt%d" % i, [128, 8, 512], F32) for i in range(1)]
        hT_ = [k.sb("phT%d" % i, [128, 8, 512], BF16) for i in range(1)]
        ycT_ = [k.sb("pyc%d" % i, [128, 8, 512], BF16) for i in range(1)]
        yaT = k.sb("yaT", [128, 4, 512], BF16)
        ybT = k.sb("ybT", [128, 4, 512], BF16)
        Rl = [k.sb("Rl%d" % i, [128, 8, 65], F32) for i in range(4)]
        rden = k.sb("rden", [128, 8, 1], F32)
        ytok = k.sb("ytok", [128, 8, 64], BF16)
        sg = [k.sb("sg%d" % i, [128, 512], F32) for i in range(3)]
        mt = k.sb("mt", [128, 512], F32)
        mt2 = k.sb("mt2", [128, 512], F32)
        mrgT = k.sb("mrgT", [128, 8, 512], BF16)
        x1T = k.sb("x1T", [128, 8, 512], F32)
        sq = mrgT
        rstd = k.sb("prstd", [128, 512], F32)
        h2T = k.sb("h2T", [128, 8, 512], BF16)
        hid = k.sb("hid", [128, 32, 512], BF16)
        x2T = x1T
        Wl = Wb_in.h.ap()[l]
        for t in range(NT):
            xt, hT, ycT = xt_[0], hT_[0], ycT_[0]
            tsl = slice(t * 512, (t + 1) * 512)
            k.dma(xt[:], fm(xsrc)[:, :, tsl])
            k.dma(hT[:], fm(hT_d)[:, :, tsl])
            k.dma(ycT[:], fm(ycT_d)[:, :, tsl])
            for j in range(4):
                blk = t * 4 + j
                rsl = slice(blk * 128, (blk + 1) * 128)
                for i in range(4):
                    k.dma(Rl[i][:, :, :].re("p a b -> p (a b)"), R_d[i][rsl, :])
                k.tt(Rl[0][:], Rl[0][:], Rl[1][:], ALU.add)
                k.tt(Rl[0][:], Rl[0][:], Rl[2][:], ALU.add)
                k.recip(rden[:], Rl[0][:, :, 64:65])
                k.tt(ytok[:], Rl[0][:, :, 0:64], V(rden, rden[:, :, :].ap.to_broadcast([128, 8, 64])), ALU.mult)
                pa = pbf(6)
                for c in range(4):
                    k.tr(pa[:, c * 128:(c + 1) * 128], ytok[:, 2 * c:2 * c + 2, :].re("p a b -> p (a b)"), identb)
                k.cp(yaT[:, :, j * 128:(j + 1) * 128], pa[:, 0:512].re("p (a b) -> p a b", a=4))
                se = V(sink_e, sink_e[:, :].ap.rearrange("p (g j) -> p j g", g=2).unsqueeze(3))
                R3 = Rl[3][:, :, :].re("p (j g) e -> p j g e", g=2)
                k.tt(R3[:, :, :, 64:65], R3[:, :, :, 64:65], se, ALU.add)
                k.recip(rden[:], Rl[3][:, :, 64:65])
                ytv = ytok[:, :, :].re("p (g j) e -> p j g e", g=2)
                k.tt(ytv, R3[:, :, :, 0:64],
                     V(rden, rden[:, :, :].ap.rearrange("p (j g) o -> p j g o", g=2).to_broadcast([128, 4, 2, 64])), ALU.mult)
                pb_ = pbf(7)
                for c in range(4):
                    k.tr(pb_[:, c * 128:(c + 1) * 128], ytok[:, 2 * c:2 * c + 2, :].re("p a b -> p (a b)"), identb)
                k.cp(ybT[:, :, j * 128:(j + 1) * 128], pb_[:, 0:512].re("p (a b) -> p a b", a=4), eng="act")
            for mg in range(2):
                wg = [wpanel_k8(V(Wb_in, Wl), 5392 + gi_ * 1024 + mg * 512, 512) for gi_ in range(3)]
                wa = wpanel_rows(V(Wb_ba, Wb_ba.h.ap()[l]), 0, 4, mg * 512, 512)
                wbb = wpanel_rows(V(Wb_bb, Wb_bb.h.ap()[l]), 0, 4, mg * 512, 512)
                wc = wpanel_k8(V(Wb_bc, Wb_bc.h.ap()[l]), mg * 512, 512)
                for mm_ in range(4):
                    m = mg * 4 + mm_
                    cs_ = slice(mm_ * 128, (mm_ + 1) * 128)
                    for gi_ in range(3):
                        pb = PB[gi_]
                        for c in range(8):
                            k.mm(pb[:, :], wg[gi_][:, c, cs_], hT[:, c, :], start=(c == 0), stop=(c == 7))
                        k.act(sg[gi_][:], pb[:, :], AF.Sigmoid, bias=biasG[:, gi_ * 8 + m:gi_ * 8 + m + 1])
                    for c in range(4):
                        k.mm(PB[3][:, :], wa[:, c, cs_], yaT[:, c, :], start=(c == 0), stop=(c == 3))
                    k.tt(mt[:], sg[0][:], PB[3][:, :], ALU.mult)
                    for c in range(4):
                        k.mm(PB[4][:, :], wbb[:, c, cs_], ybT[:, c, :], start=(c == 0), stop=(c == 3))
                    k.tt(mt2[:], sg[1][:], PB[4][:, :], ALU.mult)
                    k.tt(mt[:], mt[:], mt2[:], ALU.add, eng="pool")
                    for c in range(8):
                        k.mm(PB[5][:, :], wc[:, c, cs_], ycT[:, c, :], start=(c == 0), stop=(c == 7))
                    k.tt(mt2[:], sg[2][:], PB[5][:, :], ALU.mult)
                    k.tt(mrgT[:, m, :], mt[:], mt2[:], ALU.add, eng="pool")
            for mg in range(2):
                wo = wpanel_k8(V(Wb_out, Wb_out.h.ap()[l]), mg * 512, 512)
                for mm_ in range(4):
                    m = mg * 4 + mm_
                    pb = PB[6 + m % 2]
                    for c in range(8):
                        k.mm(pb[:, :], wo[:, c, mm_ * 128:(mm_ + 1) * 128], mrgT[:, c, :], start=(c == 0), stop=(c == 7))
                    k.tt(x1T[:, m, :], xt[:, m, :], pb[:, :], ALU.add)
            rmsnorm_T(x1T, gain2, h2T, sq, rstd, PB[0])
            for fg in range(8):
                w1 = wpanel_k8(V(Wb_f1, Wb_f1.h.ap()[l]), fg * 512, 512)
                for ff in range(4):
                    f = fg * 4 + ff
                    pb = PB[1 + f % 3]
                    for c in range(8):
                        k.mm(pb[:, :], w1[:, c, ff * 128:(ff + 1) * 128], h2T[:, c, :], start=(c == 0), stop=(c == 7))
                    k.stt(hid[:, f, :], pb[:, :], 0.0, pb[:, :], ALU.max, ALU.mult) if False else None
                    k.act(mt[:] if f % 2 == 0 else mt2[:], pb[:, :], AF.Relu)
                    src_ = mt if f % 2 == 0 else mt2
                    k.tt(hid[:, f, :], src_[:], src_[:], ALU.mult, eng=("dve" if f % 2 == 0 else "pool"))
            for m in range(8):
                pb = PB[4 + m % 4]
                for fq in range(4):
                    w2 = wpanel_rows(V(Wb_f2, Wb_f2.h.ap()[l]), fq * 1024, 8, m * 128, 128)
                    for c in range(8):
                        f = fq * 8 + c
                        k.mm(pb[:, :], w2[:, c, :], hid[:, f, :], start=(f == 0), stop=(f == 31))
                k.tt(x2T[:, m, :], x1T[:, m, :], pb[:, :], ALU.add)
            k.dma(fm(xdst)[:, :, tsl], x2T[:])

    if "L0" in phases:
        layer(0, xT[0], xT[1])
    if "L1" in phases:
        layer(1, xT[1], xT[2])

    if "F" in phases:
        k.phase_reset()
        gainF = colvec(norm_final[:], 8, "gainF")
        xt_ = [k.sb("fxt%d" % i, [128, 8, 512], F32) for i in range(2)]
        sq = k.sb("fsq", [128, 8, 512], BF16)
        rstd = k.sb("frstd", [128, 512], F32)
        yT = k.sb("fyT", [128, 8, 512], F32)
        otok = [k.sb("otok%d" % i, [128, D], F32) for i in range(2)]
        src = xT[2] if "L1" in phases else (xT[1] if "L0" in phases else xT[0])
        for t in range(NT):
            xt = xt_[t % 2]
            k.dma(xt[:], fm(src)[:, :, t * 512:(t + 1) * 512])
            k.act(sq[:, 0:4, :], xt[:, 0:4, :], AF.Square)
            k.act(sq[:, 4:8, :], xt[:, 4:8, :], AF.Square)
            for c in range(8):
                k.mm(PB[7][:, :], onesb, sq[:, c, :], start=(c == 0), stop=(c == 7))
            k.act(rstd[:], PB[7][:, :], AF.Sqrt, bias=EPS, scale=1.0 / D)
            k.recip(rstd[:], rstd[:])
            for c in range(8):
                k.stt(yT[:, c, :], xt[:, c, :], gainF[:, c:c + 1], rstd[:], ALU.mult, ALU.mult)
            for j in range(4):
                ot = otok[j % 2]
                for c in range(8):
                    k.tr(PB[c // 4][:, (c % 4) * 128:(c % 4) * 128 + 128], yT[:, c, j * 128:(j + 1) * 128], identf)
                k.cp(ot[:, 0:512], PB[0][:, :], eng="dve")
                k.cp(ot[:, 512:1024], PB[1][:, :], eng="act")
                blk = t * 4 + j
                k.dma(out[blk * 128:(blk + 1) * 128, :], ot[:])
    return k


WEIGHT_KEYS = ["w_in", "b_in", "conv_w", "a_log", "dt_bias", "sinks", "c_norm", "w_branch_a", "w_branch_b",
               "w_branch_c", "w_out", "norm_mix", "norm_ffn", "w_ff1", "w_ff2", "norm_final"]


def kernel(**inputs):
    from concourse.bass_utils import run_bass_kernel_spmd
    x = np.ascontiguousarray(inputs["x"], dtype=np.float32)
    B, S, _ = x.shape
    pos = np.ascontiguousarray(inputs["positions"], dtype=np.int32)
    cf, cb = host_consts()
    k = build(S)
    nc = k.finish()
    wts = {kk: np.ascontiguousarray(inputs[kk], dtype=np.float32) for kk in WEIGHT_KEYS}
    in_maps = []
    for core in range(8):
        b = core % B
        m = {"x": x[b], "pos": pos[b], "cf": cf, "cb": cb}
        m.update(wts)
        in_maps.append(m)
    res = run_bass_kernel_spmd(nc, in_maps, core_ids=list(range(8)))
    return np.stack([res.results[b]["out"] for b in range(B)], axis=0).astype(np.float32)
```

```python
import numpy as np
import concourse.bass as bass
import concourse.mybir as mybir
from contextlib import ExitStack

F32 = mybir.dt.float32
BF16 = mybir.dt.bfloat16
I32 = mybir.dt.int32
AF = mybir.ActivationFunctionType
ALU = mybir.AluOpType
AX = mybir.AxisListType

ENGS = ("pe", "dve", "act", "pool", "sp")
NDSEM = 8


class Tile:
    __slots__ = ("name", "h", "space", "fs", "acc", "whole")

    def __init__(self, name, h, space, fs, whole=False):
        self.name, self.h, self.space, self.fs = name, h, space, fs
        self.acc = []
        self.whole = whole

    def __getitem__(self, idx):
        return V(self, self.h[idx])

    def ap(self):
        return V(self, self.h[:])


class V:
    __slots__ = ("t", "ap")

    def __init__(self, t, ap):
        self.t, self.ap = t, ap

    def box(self):
        t = self.t
        if t.whole:
            return (0, 127, 0, 1 << 30)
        a = self.ap
        off = a.offset
        pat = a.ap
        fs = t.fs
        plo = off // fs
        flo = off % fs
        pstep, pcnt = pat[0]
        phi = plo + (pcnt - 1) * (pstep // fs if fs else 0)
        fhi = flo
        for st, cn in pat[1:]:
            if st > 0:
                fhi += (cn - 1) * st
        return (plo, phi, flo, fhi)

    def re(self, pattern, **kw):
        return V(self.t, self.ap.rearrange(pattern, **kw))

    def bc(self, shape):
        return V(self.t, self.ap.to_broadcast(shape))

    def __getitem__(self, idx):
        return V(self.t, self.ap[idx])


class Op:
    __slots__ = ("eng", "fn", "waits", "idx", "seq", "dsem", "dval", "isdma")


class K:
    def __init__(self):
        self.nc = bass.Bass("TRN2", target_bir_lowering=False)
        self.ops = []
        self.es = ExitStack()
        self.tiles = []
        self.nseq = {e: 0 for e in ENGS}
        self.ndma = {e: 0 for e in ENGS}
        self.waited = {e: {} for e in ENGS}
        self.dma_hist = {e: [] for e in ENGS}
        self.drams = {}
        self.pend = {}
        self.nosame = False

    def dram(self, name, shape, dt, kind="Internal"):
        h = self.nc.dram_tensor(name, list(shape), dt, kind=kind)
        self.drams[name] = h
        t = Tile(name, h, "dram", 0)
        return t

    def arena(self, nbytes):
        self.arena_h = self.es.enter_context(self.nc.sbuf_tensor("arena", [128, nbytes // 4], F32))
        self.arena_n = nbytes
        self.bump = 0
        self.mark = 0

    def sb(self, name, shape, dt):
        esz = {F32: 4, BF16: 2, I32: 4}[dt]
        n = int(np.prod(shape[1:]))
        nb = (n * esz + 31) // 32 * 32
        off = self.bump
        assert off + nb <= self.arena_n, ("arena overflow", name, off, nb, self.arena_n)
        self.bump = off + nb
        base = self.arena_h[0:shape[0], off // 4:(off + nb) // 4]
        if dt != F32:
            base = base.bitcast(dt)
        base = base[:, 0:n]
        if len(shape) == 3:
            base = base.rearrange("p (a b) -> p a b", a=shape[1])
        elif len(shape) == 4:
            base = base.rearrange("p (a b c) -> p a b c", a=shape[1], b=shape[2])
        t = Tile(name, base, "sb", (self.arena_n // esz))
        self.tiles.append(t)
        return t

    def set_mark(self):
        self.mark = self.bump

    def phase_reset(self):
        self.bump = self.mark
        self.barrier()

    def barrier(self):
        cur = {}
        for e in ENGS:
            if self.nseq[e] > 0:
                cur[("e", e)] = self.nseq[e]
            n = self.ndma[e]
            for kk in range(min(n, NDSEM)):
                cur[("d", e, kk)] = 16 * ((n - 1 - kk) // NDSEM + 1)
        for e in ENGS:
            pb = self.pend.setdefault(e, {})
            for key, val in cur.items():
                if key == ("e", e):
                    continue
                if pb.get(key, 0) < val:
                    pb[key] = val

    def ps(self, name, shape, dt=F32):
        h = self.es.enter_context(self.nc.psum_tensor(name, list(shape), dt))
        fs = int(np.prod(shape[1:]))
        t = Tile(name, h, "ps", fs, whole=True)
        self.tiles.append(t)
        return t

    def _deps(self, reads, writes):
        deps = set()
        for (v, w) in [(r, False) for r in reads] + [(x, True) for x in writes]:
            t = v.t
            if t.space == "dram":
                bx = self._drambox(v)
            else:
                bx = v.box()
            excl = w or t.whole
            newacc = []
            for a in t.acc:
                (oi, aw, plo, phi, flo, fhi) = a
                ov = not (bx[1] < plo or phi < bx[0] or bx[3] < flo or fhi < bx[2])
                if ov and (excl or aw):
                    deps.add(oi)
                    if excl and bx[0] <= plo and phi <= bx[1] and bx[2] <= flo and fhi <= bx[3]:
                        continue
                newacc.append(a)
            t.acc = newacc
        return deps

    def _drambox(self, v):
        a = v.ap
        off = a.offset
        hi = off
        for st, cn in a.ap:
            if st > 0:
                hi += (cn - 1) * st
        return (0, 0, off, hi)

    def _record(self, idx, reads, writes):
        for (v, w) in [(r, False) for r in reads] + [(x, True) for x in writes]:
            t = v.t
            bx = self._drambox(v) if t.space == "dram" else v.box()
            t.acc.append((idx, w or t.whole, bx[0], bx[1], bx[2], bx[3]))

    def op(self, eng, fn, reads, writes, isdma=False):
        o = Op()
        o.eng, o.fn, o.isdma = eng, fn, isdma
        o.idx = len(self.ops)
        deps = self._deps(reads, writes)
        self._record(o.idx, reads, writes)
        need = {}
        for di in deps:
            d = self.ops[di]
            if d.isdma:
                key = ("d", d.eng, d.dsem)
                val = d.dval
            else:
                if d.eng == eng and not isdma and (eng == "pe" or getattr(self, "nosame", False)):
                    continue
                key = ("e", d.eng)
                val = d.seq
            if need.get(key, 0) < val:
                need[key] = val
        if isdma:
            n = self.ndma[eng]
            self.ndma[eng] = n + 1
            o.dsem = n % NDSEM
            o.dval = 16 * (n // NDSEM + 1)
            if n >= NDSEM:
                key = ("d", eng, o.dsem)
                if need.get(key, 0) < o.dval - 16:
                    need[key] = o.dval - 16
            o.seq = 0
        else:
            self.nseq[eng] += 1
            o.seq = self.nseq[eng]
            o.dsem = o.dval = None
        pb = self.pend.pop(eng, None)
        if pb:
            for key, val in pb.items():
                if need.get(key, 0) < val:
                    need[key] = val
        w = self.waited[eng]
        o.waits = []
        for key, val in need.items():
            if w.get(key, 0) >= val:
                continue
            w[key] = val
            o.waits.append((key, val))
        self.ops.append(o)
        return o

    def _E(self, eng):
        nc = self.nc
        return {"pe": nc.tensor, "dve": nc.vector, "act": nc.scalar, "pool": nc.gpsimd, "sp": nc.sync}[eng]

    def dma(self, out, in_, eng="sp", **kw):
        self.op(eng, lambda e: e.dma_start(out=out.ap, in_=in_.ap, **kw), [in_], [out], isdma=True)

    def mm(self, out, lhsT, rhs, start=True, stop=True, **kw):
        self.op("pe", lambda e: e.matmul(out.ap, lhsT.ap, rhs.ap, start=start, stop=stop, **kw), [lhsT, rhs], [out])

    def tr(self, out, in_, ident):
        self.op("pe", lambda e: e.transpose(out.ap, in_.ap, ident.ap), [in_, ident], [out])

    def act(self, out, in_, func, bias=None, scale=1.0, eng="act", extra_reads=()):
        rd = [in_] + list(extra_reads)
        b = bias
        s = scale
        if isinstance(bias, V):
            rd.append(bias)
        if isinstance(scale, V):
            rd.append(scale)

        def f(e):
            kw = {}
            if b is not None:
                kw["bias"] = b.ap if isinstance(b, V) else b
            kw["scale"] = s.ap if isinstance(s, V) else s
            return e.activation(out.ap, in_.ap, func, **kw)
        self.op(eng, f, rd, [out])

    def tt(self, out, in0, in1, op, eng="dve"):
        self.op(eng, lambda e: e.tensor_tensor(out.ap, in0.ap, in1.ap, op), [in0, in1], [out])

    def ts(self, out, in0, s1, op0, s2=None, op1=None, eng="dve"):
        rd = [in0]
        if isinstance(s1, V):
            rd.append(s1)
        if isinstance(s2, V):
            rd.append(s2)

        def f(e):
            a1 = s1.ap if isinstance(s1, V) else s1
            a2 = s2.ap if isinstance(s2, V) else s2
            if op1 is None:
                return e.tensor_scalar(out.ap, in0.ap, a1, None, op0)
            return e.tensor_scalar(out.ap, in0.ap, a1, a2, op0, op1)
        self.op(eng, f, rd, [out])

    def stt(self, out, in0, s, in1, op0, op1, eng="dve"):
        rd = [in0, in1]
        if isinstance(s, V):
            rd.append(s)
        self.op(eng, lambda e: e.scalar_tensor_tensor(out.ap, in0.ap, s.ap if isinstance(s, V) else s, in1.ap, op0, op1), rd, [out])

    def cp(self, out, in_, eng="dve"):
        if eng == "act":
            self.op(eng, lambda e: e.copy(out.ap, in_.ap), [in_], [out])
        else:
            self.op(eng, lambda e: e.tensor_copy(out.ap, in_.ap), [in_], [out])

    def memset(self, out, val, eng="dve"):
        self.op(eng, lambda e: e.memset(out.ap, val), [], [out])

    def recip(self, out, in_):
        self.op("dve", lambda e: e.reciprocal(out.ap, in_.ap), [in_], [out])

    def finish(self, out_tiles=()):
        nc = self.nc
        final_waits = []
        for e in ENGS:
            if self.nseq[e] > 0:
                final_waits.append((("e", e), self.nseq[e]))
            n = self.ndma[e]
            for k in range(min(n, NDSEM)):
                cnt = (n - 1 - k) // NDSEM + 1
                final_waits.append((("d", e, k), 16 * cnt))
        es = self.es
        sems = {}
        for e in ENGS:
            sems[("e", e)] = es.enter_context(nc.semaphore("s_" + e))
            for k in range(NDSEM):
                sems[("d", e, k)] = es.enter_context(nc.semaphore("d_%s_%d" % (e, k)))
        per = {e: [] for e in ENGS}
        for o in self.ops:
            per[o.eng].append(o)
        blk = es.enter_context(nc.Block())

        def mk(engname):
            def body(e):
                for o in per[engname]:
                    for key, val in o.waits:
                        e.wait_ge(sems[key], val)
                    ins = o.fn(e)
                    if o.isdma:
                        ins.then_inc(sems[("d", o.eng, o.dsem)], 16)
                    else:
                        ins.then_inc(sems[("e", o.eng)], 1)
                if engname == "sp":
                    for key, val in final_waits:
                        e.wait_ge(sems[key], val)
            return body
        blk.tensor(mk("pe"))
        blk.vector(mk("dve"))
        blk.scalar(mk("act"))
        blk.gpsimd(mk("pool"))
        blk.sync(mk("sp"))
        es.close()
        return nc


D = 1024
DIN = 8464
DFF = 4096
EPS = 1e-6
NCF = 1184
NCB = 768
PI = float(np.pi)


def host_consts():
    import ml_dtypes
    p = np.arange(128)[:, None]
    f = np.arange(128)[None, :]
    same = (p // 64) == (f // 64)
    cf = np.zeros((128, NCF), np.float32)
    cf[:, 0:128] = np.eye(128)
    cf[:, 128:256] = ((p <= f) & same)
    cf[:, 256:384] = (p == 64 * (f // 64) + 63)
    cf[:, 384:512] = (p == 63)
    cf[:, 512:640] = (p == 127)
    cf[:, 640:768] = ((f > p) & same)
    cf[:, 768:896] = ((f >= p) & same)
    cf[:, 896:1024] = ((p > f) & same)
    cf[:, 1024:1152] = 1.0
    invf = (500000.0 ** (-np.arange(0, 16, 2, dtype=np.float32) / 16)).astype(np.float32)
    cf[:, 1152:1160] = invf[None, :]
    cf[:, 1160:1168] = invf[None, :]
    cf[:, 1168:1176] = 0.0
    cf[:, 1176:1184] = np.pi / 2
    cb = np.zeros((128, NCB), np.float32)
    cb[:, 0:128] = np.eye(128)
    cb[:, 128:256] = 1.0
    cb[:, 256:384] = (p >= f)
    cb[:, 384:512] = (p <= f)
    cb[:, 512:640] = (p > f)
    cb[:, 640:768] = (p <= f)
    return cf, cb.astype(ml_dtypes.bfloat16)


def build(S, dbg=False, phases=("W", "X", "L0", "L1", "F"), sub=("P1", "P2", "P2b", "P3")):
    k = K()
    NB = S // 128
    NT = S // 512
    EXT_I, EXT_O = "ExternalInput", "ExternalOutput"
    SCR = EXT_O if dbg else "Internal"
    x_in = k.dram("x", [S, D], F32, EXT_I)
    pos_in = k.dram("pos", [S], I32, EXT_I)
    cf_in = k.dram("cf", [128, NCF], F32, EXT_I)
    cb_in = k.dram("cb", [128, NCB], BF16, EXT_I)
    w_in = k.dram("w_in", [2, D, DIN], F32, EXT_I)
    b_in = k.dram("b_in", [2, DIN], F32, EXT_I)
    conv_w = k.dram("conv_w", [2, 4, 2048], F32, EXT_I)
    a_log = k.dram("a_log", [2, 8], F32, EXT_I)
    dt_bias = k.dram("dt_bias", [2, 8], F32, EXT_I)
    sinks = k.dram("sinks", [2, 8], F32, EXT_I)
    c_norm = k.dram("c_norm", [2, 128], F32, EXT_I)
    w_ba = k.dram("w_branch_a", [2, 512, D], F32, EXT_I)
    w_bb = k.dram("w_branch_b", [2, 512, D], F32, EXT_I)
    w_bc = k.dram("w_branch_c", [2, D, D], F32, EXT_I)
    w_out = k.dram("w_out", [2, D, D], F32, EXT_I)
    norm_mix = k.dram("norm_mix", [2, D], F32, EXT_I)
    norm_ffn = k.dram("norm_ffn", [2, D], F32, EXT_I)
    w_ff1 = k.dram("w_ff1", [2, D, DFF], F32, EXT_I)
    w_ff2 = k.dram("w_ff2", [2, DFF, D], F32, EXT_I)
    norm_final = k.dram("norm_final", [D], F32, EXT_I)
    out = k.dram("out", [S, D], F32, EXT_O)
    Wb_in = k.dram("Wb_in", [2, D, DIN], BF16, "Internal")
    Wb_ba = k.dram("Wb_ba", [2, 512, D], BF16, "Internal")
    Wb_bb = k.dram("Wb_bb", [2, 512, D], BF16, "Internal")
    Wb_bc = k.dram("Wb_bc", [2, D, D], BF16, "Internal")
    Wb_out = k.dram("Wb_out", [2, D, D], BF16, "Internal")
    Wb_f1 = k.dram("Wb_f1", [2, D, DFF], BF16, "Internal")
    Wb_f2 = k.dram("Wb_f2", [2, DFF, D], BF16, "Internal")
    xT = [k.dram("xT%d" % i, [D, S], F32, SCR) for i in range(3)]
    hT_d = k.dram("hT_d", [D, S], BF16, SCR)
    qkv_d = k.dram("qkv_d", [S, 2304], BF16, SCR)
    R_d = [k.dram("R%d" % i, [S, 520], F32, SCR) for i in range(4)]
    ycT_d = k.dram("ycT_d", [D, S], BF16, SCR)

    k.arena(198 * 1024)
    PB = [k.ps("PB%d" % i, [128, 512]) for i in range(8)]

    def pbf(i):
        return V(PB[i], PB[i].h[:].bitcast(BF16))

    cf = k.sb("cf", [128, NCF], F32)
    cb = k.sb("cb", [128, NCB], BF16)
    k.dma(cf[:], cf_in[:])
    k.dma(cb[:], cb_in[:])
    identf = cf[:, 0:128]
    tri = cf[:, 128:256]
    sellastblk = cf[:, 256:384]
    sellast = [cf[:, 384:512], cf[:, 512:640]]
    maskU = cf[:, 640:768]
    maskUi = cf[:, 768:896]
    maskL = cf[:, 896:1024]
    onesf = cf[:, 1024:1152]
    invf = cf[:, 1152:1168]
    phase = cf[:, 1168:1184]
    identb = cb[:, 0:128]
    onesb = cb[:, 128:256]
    maskA = cb[:, 256:512]
    maskB = cb[:, 512:768]
    cs = k.sb("cs", [128, NB, 16], F32)
    k.set_mark()

    def bcast_mid(v, n):
        return V(v.t, v.ap.unsqueeze(1).to_broadcast([v.ap.shape[0], n, v.ap.shape[1]]))

    def bcast_last(v, n):
        return V(v.t, v.ap.unsqueeze(2).to_broadcast([v.ap.shape[0], v.ap.shape[1], n]))

    def colvec(dram_v, nchunks, name):
        t = k.sb(name, [128, nchunks], F32)
        k.dma(t[:], V(dram_v.t, dram_v.ap.rearrange("(c p) -> p c", p=128)), allow_slow_non_contiguous=True)
        return t

    def rowbc(dram_v, n, name):
        t = k.sb(name, [128, n], F32)
        k.dma(t[:], V(dram_v.t, dram_v.ap.partition_broadcast(128)))
        return t

    rr = {"i": 0}

    def rot_eng(engs=("dve", "act", "pool")):
        rr["i"] += 1
        return engs[rr["i"] % len(engs)]

    def cast_matrix(src, dst, R, C):
        st32 = [k.sb("st32_%d" % i, [128, 2048], F32) for i in range(3)]
        st16 = [k.sb("st16_%d" % i, [128, 2048], BF16) for i in range(3)]
        i = 0
        for r0 in range(0, R, 128):
            for c0 in range(0, C, 2048):
                cw = min(2048, C - c0)
                a, b = st32[i % 3], st16[i % 3]
                k.dma(a[:, 0:cw], src[r0:r0 + 128, c0:c0 + cw])
                k.cp(b[:, 0:cw], a[:, 0:cw], eng=rot_eng())
                k.dma(dst[r0:r0 + 128, c0:c0 + cw], b[:, 0:cw])
                i += 1

    if "W" in phases:
        k.phase_reset()
        st32 = [k.sb("st32_%d" % i, [128, 2048], F32) for i in range(3)]
        st16 = [k.sb("st16_%d" % i, [128, 2048], BF16) for i in range(3)]
        cnt = [0]

        def castm(src, dst, R, C):
            for r0 in range(0, R, 128):
                for c0 in range(0, C, 2048):
                    cw = min(2048, C - c0)
                    a, b = st32[cnt[0] % 3], st16[cnt[0] % 3]
                    k.dma(a[:, 0:cw], src[r0:r0 + 128, c0:c0 + cw])
                    k.cp(b[:, 0:cw], a[:, 0:cw], eng=rot_eng())
                    k.dma(dst[r0:r0 + 128, c0:c0 + cw], b[:, 0:cw])
                    cnt[0] += 1
        for l in range(1):
            castm(w_in[l], Wb_in[l], D, DIN)
            castm(w_ba[l], Wb_ba[l], 512, D)
            castm(w_bb[l], Wb_bb[l], 512, D)
            castm(w_bc[l], Wb_bc[l], D, D)
            castm(w_out[l], Wb_out[l], D, D)
            castm(w_ff1[l], Wb_f1[l], D, DFF)
            castm(w_ff2[l], Wb_f2[l], DFF, D)

    def deferred_casts(l, st32, st16):
        cnt = [0]
        for (src, dst, R_, C_) in ((w_in[l], Wb_in[l], D, DIN), (w_ba[l], Wb_ba[l], 512, D), (w_bb[l], Wb_bb[l], 512, D),
                                   (w_bc[l], Wb_bc[l], D, D), (w_out[l], Wb_out[l], D, D), (w_ff1[l], Wb_f1[l], D, DFF),
                                   (w_ff2[l], Wb_f2[l], DFF, D)):
            for r0 in range(0, R_, 128):
                for c0 in range(0, C_, 2048):
                    cw = min(2048, C_ - c0)
                    a, b = st32[cnt[0] % len(st32)], st16[cnt[0] % len(st16)]
                    k.dma(a[:, 0:cw], src[r0:r0 + 128, c0:c0 + cw])
                    k.cp(b[:, 0:cw], a[:, 0:cw], eng=("act" if cnt[0] % 2 == 0 else "dve"))
                    k.dma(dst[r0:r0 + 128, c0:c0 + cw], b[:, 0:cw])
                    cnt[0] += 1
                    yield

    def fm(dram_t):
        return V(dram_t, dram_t.h.ap().rearrange("(c p) s -> p c s", p=128))

    if "X" in phases:
        k.phase_reset()
        xtok = [k.sb("xtok%d" % i, [128, D], F32) for i in range(2)]
        xTt = [k.sb("xTt%d" % i, [128, 8, 512], F32) for i in range(2)]
        for blk in range(NB):
            a = xtok[blk % 2]
            k.dma(a[:], x_in[blk * 128:(blk + 1) * 128, :])
            tt_ = xTt[(blk // 4) % 2]
            for c in range(8):
                k.tr(PB[c // 4][:, (c % 4) * 128:(c % 4) * 128 + 128], a[:, c * 128:(c + 1) * 128], identf)
            j = blk % 4
            k.cp(tt_[:, 0:4, j * 128:(j + 1) * 128], PB[0][:, :].re("p (a b) -> p a b", a=4), eng="dve")
            k.cp(tt_[:, 4:8, j * 128:(j + 1) * 128], PB[1][:, :].re("p (a b) -> p a b", a=4), eng="act")
            if j == 3:
                t = blk // 4
                k.dma(fm(xT[0])[:, :, t * 512:(t + 1) * 512], tt_[:])
        posi = k.sb("posi", [128, NB], I32)
        posf = k.sb("posf", [128, NB], F32)
        ang = k.sb("ang", [128, NB, 16], F32)
        ki = k.sb("ki", [128, NB, 16], I32)
        kf = k.sb("kf", [128, NB, 16], F32)
        msk = k.sb("msk", [128, NB, 16], F32)
        k.dma(posi[:], V(pos_in, pos_in.h.ap().rearrange("(n p) -> p n", p=128)), allow_slow_non_contiguous=True)
        k.cp(posf[:], posi[:])
        k.tt(ang[:], bcast_last(posf[:], 16), bcast_mid(invf, NB), ALU.mult)
        k.tt(ang[:], ang[:], bcast_mid(phase, NB), ALU.add)
        k.ts(kf[:], ang[:], 1.0 / (2 * PI), ALU.mult)
        k.cp(ki[:], kf[:])
        k.cp(kf[:], ki[:])
        k.stt(ang[:], kf[:], -2 * PI, ang[:], ALU.mult, ALU.add)
        k.ts(msk[:], ang[:], PI, ALU.is_gt)
        k.stt(ang[:], msk[:], -2 * PI, ang[:], ALU.mult, ALU.add)
        k.ts(msk[:], ang[:], -PI, ALU.is_lt)
        k.stt(ang[:], msk[:], 2 * PI, ang[:], ALU.mult, ALU.add)
        k.ts(ang[:], ang[:], PI, ALU.min, -PI, ALU.max)
        k.act(cs[:], ang[:], AF.Sin)
        if dbg:
            cs_d = k.dram("cs_d", [128, NB * 16], F32, EXT_O)
            k.dma(cs_d[:], cs[:, :, :].re("p a b -> p (a b)"))

    def rmsnorm_T(xt, gain, hT, sq, rstd, pbank):
        k.act(sq[:, 0:4, :], xt[:, 0:4, :], AF.Square)
        k.act(sq[:, 4:8, :], xt[:, 4:8, :], AF.Square)
        for c in range(8):
            k.mm(pbank[:, :], onesb, sq[:, c, :], start=(c == 0), stop=(c == 7))
        k.act(rstd[:], pbank[:, :], AF.Sqrt, bias=EPS, scale=1.0 / D)
        k.recip(rstd[:], rstd[:])
        for c in range(8):
            k.stt(hT[:, c, :], xt[:, c, :], gain[:, c:c + 1], rstd[:], ALU.mult, ALU.mult)

    def layer(l, xsrc, xdst):
        if "P1" in sub:
            proj1(l, xsrc)
        if "P2" in sub:
            attn(l)
        if "P2b" in sub:
            k.phase_reset()
            deltanet(l)
        if "P3" in sub:
            k.phase_reset()
            post(l, xsrc, xdst)

    def proj1(l, xsrc):
        k.phase_reset()
        W_AB = k.sb("W_AB", [128, 8, 2304], BF16)
        k.dma(W_AB[:], V(Wb_in, Wb_in.h.ap()[l].rearrange("(c p) n -> p c n", p=128)[:, :, 0:2304]))
        biasAB = rowbc(b_in[l:l + 1, 0:2304], 2304, "biasAB")
        gain = colvec(norm_mix[l], 8, "gain")
        xt_ = [k.sb("xt%d" % i, [128, 8, 512], F32) for i in range(2)]
        sq = k.sb("sq", [128, 8, 512], BF16)
        rstd = k.sb("rstd", [128, 512], F32)
        hT_ = [k.sb("hT%d" % i, [128, 8, 512], BF16) for i in range(2)]
        u = k.sb("u", [128, 2304], F32)
        ub_ = [k.sb("ub%d" % i, [128, 2304], BF16) for i in range(2)]
        rt = [k.sb("rt%d" % i, [128, 16, 8], F32) for i in range(4)]
        for t in range(NT):
            xt, hT = xt_[t % 2], hT_[t % 2]
            k.dma(xt[:], fm(xsrc)[:, :, t * 512:(t + 1) * 512])
            rmsnorm_T(xt, gain, hT, sq, rstd, PB[7])
            k.dma(fm(hT_d)[:, :, t * 512:(t + 1) * 512], hT[:])
            for j in range(4):
                blk = t * 4 + j
                ub = ub_[blk % 2]
                for cg in range(5):
                    w = 512 if cg < 4 else 256
                    pb = PB[cg % 4]
                    for c in range(8):
                        k.mm(pb[:, 0:w], hT[:, c, j * 128:(j + 1) * 128], W_AB[:, c, cg * 512:cg * 512 + w],
                             start=(c == 0), stop=(c == 7))
                    k.tt(u[:, cg * 512:cg * 512 + w], pb[:, 0:w], biasAB[:, cg * 512:cg * 512 + w], ALU.add)
                k.cp(ub[:], u[:], eng="act")
                uv = u[:, :].re("p (h d) -> p h d", d=64)
                ubv = ub[:, :].re("p (h d) -> p h d", d=64)
                for (h0, nh) in ((0, 16), (24, 10)):
                    x1 = uv[:, h0:h0 + nh, 0:8]
                    x2 = uv[:, h0:h0 + nh, 8:16]
                    cosb = bcast_mid(cs[:, blk, 8:16], nh)
                    sinb = bcast_mid(cs[:, blk, 0:8], nh)
                    t1, t2, t3, t4 = (r_[:, 0:nh, :] for r_ in rt)
                    k.tt(t1, x1, cosb, ALU.mult)
                    k.tt(t2, x2, sinb, ALU.mult, eng="pool")
                    k.tt(t3, x2, cosb, ALU.mult)
                    k.tt(t4, x1, sinb, ALU.mult, eng="pool")
                    k.tt(ubv[:, h0:h0 + nh, 0:8], t1, t2, ALU.subtract)
                    k.tt(ubv[:, h0:h0 + nh, 8:16], t3, t4, ALU.add, eng="pool")
                k.dma(qkv_d[blk * 128:(blk + 1) * 128, :], ub[:])

    def attn(l):
        k.phase_reset()
        NQ = 3
        Qb_ = [k.sb("Qb%d" % i, [128, 512], BF16) for i in range(NQ)]
        Kb_ = [k.sb("Kb%d" % i, [128, 512], BF16) for i in range(NQ)]
        Vb_ = [k.sb("Vb%d" % i, [128, 8, 72], BF16) for i in range(4)]
        QT_ = [k.sb("QT%d" % i, [128, 2, 4, 128], BF16) for i in range(NQ)]
        for qt in QT_:
            k.memset(qt[64:128, 0, :, :], 0.0)
            k.memset(qt[0:64, 1, :, :], 0.0, eng="pool")
        KT_ = [k.sb("KT%d" % i, [128, 4, 128], BF16) for i in range(4)]
        PT_ = [[k.sb("PT%d_%d" % (s_, i), [128, 2, 2, 128], BF16) for i in range(4)] for s_ in range(2)]
        O_ = [k.sb("O%d" % i, [128, 8, 65], F32) for i in range(2)]
        for vb in Vb_:
            k.memset(vb[:, :, 64:65], 1.0)
        dgen = None
        if l == 0 and "W" in phases and "L1" in phases:
            dst32 = [k.sb("dst32_%d" % i, [128, 2048], F32) for i in range(3)]
            dst16 = [k.sb("dst16_%d" % i, [128, 2048], BF16) for i in range(3)]
            dgen = deferred_casts(1, dst32, dst16)
        groups = []
        for (ci, d, qoff, koff, voff, nkv, isB) in ((0, 1, 0, 512, 1024, 8, False), (1, 4, 0, 512, 1024, 8, False),
                                                     (2, 16, 0, 512, 1024, 8, False), (3, 1, 1536, 2048, 2176, 2, True)):
            L = S // d
            for r in range(d):
                for n in range(L // 128):
                    groups.append((ci, d, qoff, koff, voff, nkv, isB, r, n))
        G = len(groups)

        def rows_of(g):
            (ci, d, qoff, koff, voff, nkv, isB, r, n) = groups[g]
            lo = r + d * 128 * n
            hi = lo + d * 127 + 1
            return lo, hi

        def stage1(g):
            (ci, d, qoff, koff, voff, nkv, isB, r, n) = groups[g]
            nkc = 4 if not isB else 1
            lo, hi = rows_of(g)
            rows = qkv_d.h.ap()[lo:hi:d, :]
            Qb, Kb, Vb = Qb_[g % NQ], Kb_[g % NQ], Vb_[g % 4]
            QT, KT = QT_[g % NQ], KT_[g % 4]
            if not isB:
                k.dma(Qb[:], V(qkv_d, rows[:, qoff:qoff + 512]))
            else:
                for g_ in range(2):
                    k.dma(Qb[:, :].re("p (j g e) -> p j g e", j=4, g=2)[:, :, g_, :],
                          V(qkv_d, rows[:, qoff + g_ * 256:qoff + g_ * 256 + 256].rearrange("t (j e) -> t j e", j=4)))
            k.dma(Kb[:, 0:nkc * 128], V(qkv_d, rows[:, koff:koff + nkc * 128]))
            k.dma(Vb[:, 0:nkv, 0:64], V(qkv_d, rows[:, voff:voff + nkv * 64].rearrange("t (h e) -> t h e", e=64)))
            pq, pk = pbf(6), pbf(7)
            for c in range(4):
                k.tr(pq[:, c * 128:(c + 1) * 128], Qb[:, c * 128:(c + 1) * 128], identb)
            k.cp(QT[0:64, 0, :, :], pq[0:64, 0:512].re("p (a b) -> p a b", a=4), eng="dve")
            k.cp(QT[64:128, 1, :, :], pq[64:128, 0:512].re("p (a b) -> p a b", a=4), eng="dve")
            for c in range(nkc):
                k.tr(pk[:, c * 128:(c + 1) * 128], Kb[:, c * 128:(c + 1) * 128], identb)
            k.cp(KT[:, 0:nkc, :], pk[:, 0:nkc * 128].re("p (a b) -> p a b", a=nkc), eng="act")

        def stage2(g):
            (ci, d, qoff, koff, voff, nkv, isB, r, n) = groups[g]
            mask = maskB if isB else maskA
            mask4 = V(mask.t, mask.ap.rearrange("p (a b) -> p a b", a=2).unsqueeze(1).to_broadcast([128, 2, 2, 128]))
            QT, KT, KTp = QT_[g % NQ], KT_[g % 4], KT_[(g - 1) % 4]
            kcs = (0, 1) if n > 0 else (1,)
            ksl = slice(0, 2) if n > 0 else slice(1, 2)
            for half in range(2):
                for bq in range(2):
                    pb = PB[half * 2 + bq]
                    PT = PT_[g % 2][half * 2 + bq]
                    for hh in range(2):
                        h = half * 4 + bq + 2 * hh
                        for kc in kcs:
                            src = KTp if kc == 0 else KT
                            kch = (h // 2) if not isB else 0
                            col = (hh * 2 + kc) * 128
                            k.mm(pb[:, col:col + 128], src[:, kch, :], QT[:, h % 2, h // 2, :])
                    k.act(PT[:, 0:2, ksl, :], pb[:, :].re("p (a b c) -> p a b c", a=2, b=2)[:, :, ksl, :], AF.Exp, scale=0.125)
                    k.tt(PT[:, 0:2, ksl, :], PT[:, 0:2, ksl, :], mask4[:, :, ksl, :], ALU.mult, eng=("dve" if bq == 0 else "pool"))

        def stage3(g):
            (ci, d, qoff, koff, voff, nkv, isB, r, n) = groups[g]
            lo, hi = rows_of(g)
            Vb, Vp = Vb_[g % 4], Vb_[(g - 1) % 4]
            O = O_[g % 2]
            kcs = (0, 1) if n > 0 else (1,)
            for half in range(2):
                po = PB[4 + half]
                for bq in range(2):
                    PT = PT_[g % 2][half * 2 + bq]
                    for hh in range(2):
                        h = half * 4 + bq + 2 * hh
                        hs = h - half * 4
                        kvh = h if not isB else (h % 2)
                        for kc in kcs:
                            vsrc = Vp if kc == 0 else Vb
                            k.mm(po[:, hs * 65:hs * 65 + 65], PT[:, hh, kc, :], vsrc[:, kvh, 0:65],
                                 start=(kc == kcs[0]), stop=(kc == 1))
                k.cp(O[:, half * 4:half * 4 + 4, :], po[:, 0:260].re("p (a b) -> p a b", a=4),
                     eng=("dve" if half == 0 else "act"))
            k.dma(V(R_d[ci], R_d[ci].h.ap()[lo:hi:d, :]), O[:, :, :].re("p a b -> p (a b)"))

        for s_ in range(G + 2):
            if s_ < G:
                stage1(s_)
            if 0 <= s_ - 1 < G:
                stage2(s_ - 1)
            if 0 <= s_ - 2 < G:
                stage3(s_ - 2)
            if dgen is not None and (s_ % max(1, (G // 80)) == 0 or s_ >= G):
                try:
                    next(dgen)
                except StopIteration:
                    dgen = None
        while dgen is not None:
            try:
                next(dgen)
            except StopIteration:
                dgen = None

    def deltanet(l):
        W_C = k.sb("W_C", [128, 8, 3088], BF16)
        k.dma(W_C[:], V(Wb_in, Wb_in.h.ap()[l].rearrange("(c p) n -> p c n", p=128)[:, :, 2304:5392]))
        biasC = colvec(b_in[l, 2304:5376], 24, "biasC")
        biasab = rowbc(b_in[l:l + 1, 5376:5392], 16, "biasab")
        convw = k.sb("convw", [128, 16, 4], F32)
        for j_ in range(4):
            k.dma(convw[:, :, j_], V(conv_w, conv_w.h.ap()[l, j_].rearrange("(c p) -> p c", p=128)), allow_slow_non_contiguous=True)
        alog = rowbc(a_log[l:l + 1, :], 8, "alog")
        dtb = rowbc(dt_bias[l:l + 1, :], 8, "dtb")
        cnorm = colvec(c_norm[l], 1, "cnorm")
        nalog = k.sb("nalog", [128, 8], F32)
        k.act(nalog[:], alog[:], AF.Exp)
        k.ts(nalog[:], nalog[:], -1.0, ALU.mult)
        hT_ = [k.sb("dhT%d" % i, [128, 8, 512], BF16) for i in range(1)]
        U_ = [k.sb("U%d" % i, [128, 515], F32) for i in range(1)]
        carry = k.sb("carry", [128, 16, 3], F32)
        k.memset(carry[:], 0.0)
        ytmp = [k.sb("ytmp%d" % i, [128, 512], F32) for i in range(1)]
        cvo = k.sb("cvo", [128, 512], F32)
        sqc = k.sb("sqc", [128, 512], BF16)
        rs = k.sb("rs", [128, 512], F32)
        qkT = k.sb("qkT", [128, 8, 512], BF16)
        vT = k.sb("vT", [128, 8, 512], BF16)
        zs1 = k.sb("zs1", [128, 512], BF16)
        ab = k.sb("ab", [128, 4, 16], F32)
        xa = k.sb("xa", [128, 4, 8], F32)
        t8a = k.sb("t8a", [128, 4, 8], F32)
        t8b = k.sb("t8b", [128, 4, 8], F32)
        g_ = k.sb("g_", [128, 4, 8], F32)
        beta_ = k.sb("beta_", [128, 4, 8], F32)
        sc = k.sb("sc", [128, 16], F32)
        gcl = k.sb("gcl", [128, 8], F32)
        kd = k.sb("kd", [128, 8], F32)
        eg = k.sb("eg", [128, 8], F32)
        bk = k.sb("bk", [128, 8], F32)
        gl2 = [[k.sb("gl%d_%d" % (s_, c), [128, 8], F32) for c in range(2)] for s_ in range(2)]
        Dm = k.sb("Dm", [128, 8, 128], F32)
        diff1 = k.sb("diff1", [128, 8, 128], F32)
        tA = Dm
        dT = k.sb("dT", [128, 8, 128], BF16)
        d2 = k.sb("d2", [128, 8, 128], BF16)
        Re2 = [k.sb("Re%d" % i, [128, 8, 128], F32) for i in range(2)]
        Rb = k.sb("Rb", [128, 8, 128], BF16)
        kkt_s = k.sb("kkt_s", [128, 4, 128], BF16)
        qk_s = k.sb("qk_s", [128, 4, 128], BF16)
        Xb = [k.sb("Xb%d" % i, [128, 8, 128], BF16) for i in range(2)]
        Yb = [k.sb("Yb%d" % i, [128, 8, 128], BF16) for i in range(2)]
        Qb2 = [k.sb("Qq%d" % i, [128, 8, 128], BF16) for i in range(2)]
        attnT2 = [k.sb("attnT%d" % i, [128, 8, 128], BF16) for i in range(2)]
        TT2 = [k.sb("TT%d" % i, [128, 8, 128], BF16) for i in range(2)]
        ktok = k.sb("ktok", [128, 4, 128], BF16)
        vtok = k.sb("vtok", [128, 8, 128], BF16)
        kb_ = k.sb("kb_", [128, 8, 128], BF16)
        vb2 = [k.sb("vb_%d" % i, [128, 8, 128], BF16) for i in range(2)]
        kdec2 = [k.sb("kdec%d" % i, [128, 8, 128], BF16) for i in range(2)]
        nwT2 = [k.sb("nwT%d" % i, [128, 8, 128], BF16) for i in range(2)]
        vn = k.sb("vn", [128, 8, 128], BF16)
        Sf = k.sb("Sf", [128, 8, 128], F32)
        Sb = k.sb("Sb", [128, 8, 128], BF16)
        t1 = k.sb("t1", [128, 8, 64], F32)
        oT = k.sb("oT", [128, 8, 512], F32)
        ycT_ = [k.sb("ycT%d" % i, [128, 512], BF16) for i in range(2)]
        k.memset(Sf[:], 0.0)
        k.memset(Sb[:], 0.0)
        k.memset(vn[:], 0.0)

        def v4(t):
            return t[:, :, :].re("p (a b) c -> p a b c", a=4)

        def b4(t):
            return V(t, t[:, :, :].ap.unsqueeze(2).to_broadcast([128, 4, 2, 128]))

        def pb2(i0):
            return PB[i0], PB[i0 + 1]

        for t in range(NT):
            hT = hT_[0]
            k.dma(hT[:], fm(hT_d)[:, :, t * 512:(t + 1) * 512])
            for cc in range(16):
                pb = PB[cc % 4]
                U = U_[0]
                for c in range(8):
                    k.mm(pb[:, :], W_C[:, c, cc * 128:(cc + 1) * 128], hT[:, c, :], start=(c == 0), stop=(c == 7))
                k.cp(U[:, 0:3], carry[:, cc, :], eng="pool")
                k.act(U[:, 3:515], pb[:, :], AF.Identity, bias=biasC[:, cc:cc + 1])
                y = ytmp[0]
                k.ts(y[:], U[:, 3:515], convw[:, cc, 3:4], ALU.mult)
                k.stt(y[:], U[:, 2:514], convw[:, cc, 2:3], y[:], ALU.mult, ALU.add)
                k.stt(y[:], U[:, 1:513], convw[:, cc, 1:2], y[:], ALU.mult, ALU.add)
                k.stt(y[:], U[:, 0:512], convw[:, cc, 0:1], y[:], ALU.mult, ALU.add)
                k.cp(carry[:, cc, :], U[:, 512:515], eng="pool")
                if cc < 8:
                    k.act(cvo[:], y[:], AF.Silu)
                    k.act(sqc[:], cvo[:], AF.Square)
                    k.mm(PB[4 + cc % 2][:, :], onesb, sqc[:])
                    k.act(rs[:], PB[4 + cc % 2][:, :], AF.Sqrt, bias=EPS, scale=1.0)
                    k.recip(rs[:], rs[:])
                    k.stt(qkT[:, cc, :], cvo[:], (128.0 ** -0.5) if cc < 4 else 1.0, rs[:], ALU.mult, ALU.mult)
                else:
                    k.act(vT[:, cc - 8, :], y[:], AF.Silu)
            for j in range(4):
                for c in range(8):
                    k.mm(PB[6][:, j * 16:(j + 1) * 16], hT[:, c, j * 128:(j + 1) * 128], W_C[:, c, 3072:3088],
                         start=(c == 0), stop=(c == 7))
            k.tt(ab[:], PB[6][:, 0:64].re("p (a b) -> p a b", a=4), bcast_mid(biasab[:, :], 4), ALU.add)
            k.tt(xa[:], ab[:, :, 0:8], bcast_mid(dtb[:, :], 4), ALU.add)
            k.ts(t8a[:], xa[:], -1.0, ALU.mult)
            k.tt(t8a[:], t8a[:], xa[:], ALU.max)
            k.act(t8a[:], t8a[:], AF.Exp, scale=-1.0)
            k.act(t8a[:], t8a[:], AF.Ln, bias=1.0)
            k.stt(t8b[:], xa[:], 0.0, t8a[:], ALU.max, ALU.add)
            k.tt(g_[:], t8b[:], bcast_mid(nalog[:, :], 4), ALU.mult)
            k.act(beta_[:], ab[:, :, 8:16], AF.Sigmoid)
            def pre(j, t=t):
                bc0 = j * 128
                cols = slice(bc0, bc0 + 128)
                gblk = t * 4 + j
                sidx = gblk % 2
                gl, Re, attnT, TT = gl2[sidx], Re2[sidx], attnT2[sidx], TT2[sidx]
                vb_, kdec, nwT = vb2[sidx], kdec2[sidx], nwT2[sidx]
                gb = g_[:, j, :]
                betab = beta_[:, j, :]
                pk_, pv_ = pbf(0), pbf(1)
                for hq in range(4):
                    k.tr(pk_[:, hq * 128:(hq + 1) * 128], qkT[:, 4 + hq, cols], identb)
                k.cp(ktok[:], pk_[:, 0:512].re("p (a b) -> p a b", a=4))
                for h in range(8):
                    k.tr(pv_[:, h * 128:(h + 1) * 128], vT[:, h, cols], identb)
                k.cp(vtok[:], pv_[:, 0:1024].re("p (a b) -> p a b", a=8), eng="act")
                yield
                k.mm(PB[2][:, 0:8], tri, gb)
                k.cp(sc[:, 0:8], PB[2][:, 0:8])
                k.cp(sc[:, 8:16], betab, eng="pool")
                gc = sc[:, 0:8]
                k.mm(PB[2][:, 8:16], sellastblk, gc)
                k.mm(PB[2][:, 16:24], sellast[0], gc)
                k.mm(PB[2][:, 24:32], sellast[1], gc)
                k.tt(gcl[:], PB[2][:, 8:16], gc, ALU.subtract)
                k.act(kd[:], gcl[:], AF.Exp)
                k.act(gl[0][:], PB[2][:, 16:24], AF.Exp)
                k.act(gl[1][:], PB[2][:, 24:32], AF.Exp)
                k.act(eg[:], gc, AF.Exp)
                k.tt(bk[:], betab, eg[:], ALU.mult)
                k.tt(v4(kb_), b4(ktok), bcast_last(bk[:, :], 128).re("p (a b) c -> p a b c", a=4), ALU.mult)
                k.tt(vb_[:], vtok[:], bcast_last(sc[:, 8:16], 128), ALU.mult, eng="pool")
                k.tt(v4(kdec), b4(ktok), bcast_last(kd[:, :], 128).re("p (a b) c -> p a b c", a=4), ALU.mult)
                yield
                k.tt(Dm[:], bcast_mid(identf, 8), bcast_last(sc[:, 0:8], 128), ALU.mult)
                for q in range(2):
                    k.mm(PB[q][:, :], onesf, Dm[:, 4 * q:4 * q + 4, :].re("p a b -> p (a b)"))
                k.tt(Dm[:], bcast_mid(identf, 8), bcast_last(sc[:, 8:16], 128), ALU.mult)
                for q in range(2):
                    k.mm(PB[2 + q][:, :], onesf, Dm[:, 4 * q:4 * q + 4, :].re("p a b -> p (a b)"))
                for hb in range(2):
                    k.tt(diff1[:, 4 * hb:4 * hb + 4, :], PB[hb][:, :].re("p (a b) -> p a b", a=4),
                         bcast_last(sc[:, 4 * hb:4 * hb + 4], 128), ALU.subtract)
                    k.act(Re[:, 4 * hb:4 * hb + 4, :], PB[hb][:, :].re("p (a b) -> p a b", a=4), AF.Exp)
                    k.cp(Rb[:, 4 * hb:4 * hb + 4, :], PB[2 + hb][:, :].re("p (a b) -> p a b", a=4), eng="act")
                yield
                k.ts(tA[:], diff1[:], 0.0, ALU.min)
                k.act(dT[:], tA[:], AF.Exp)
                k.ts(tA[:], diff1[:], -1.0, ALU.mult, 0.0, ALU.min)
                k.act(d2[:], tA[:], AF.Exp)
                for hq in range(4):
                    k.mm(PB[0][:, hq * 128:(hq + 1) * 128], qkT[:, 4 + hq, cols], qkT[:, 4 + hq, cols])
                for hq in range(4):
                    k.mm(PB[1][:, hq * 128:(hq + 1) * 128], qkT[:, 4 + hq, cols], qkT[:, hq, cols])
                k.cp(kkt_s[:], PB[0][:, :].re("p (a b) -> p a b", a=4))
                k.cp(qk_s[:], PB[1][:, :].re("p (a b) -> p a b", a=4), eng="act")
                yield
                X0, Y0, Q0 = Xb[0], Yb[0], Qb2[0]
                k.tt(tA[:], dT[:], bcast_mid(maskU, 8), ALU.mult)
                k.tt(tA[:], tA[:], Rb[:], ALU.mult)
                k.tt(v4(tA), v4(tA), b4(kkt_s), ALU.mult)
                k.ts(X0[:], tA[:], -1.0, ALU.mult)
                yield
                k.tt(tA[:], dT[:], bcast_mid(maskUi, 8), ALU.mult)
                k.tt(v4(attnT), v4(tA), b4(qk_s), ALU.mult)
                yield
                k.tt(tA[:], d2[:], bcast_mid(maskL, 8), ALU.mult)
                k.tt(tA[:], tA[:], bcast_last(sc[:, 8:16], 128), ALU.mult)
                k.tt(v4(tA), v4(tA), b4(kkt_s), ALU.mult)
                k.ts(Y0[:], tA[:], -1.0, ALU.mult)
                k.tt(Q0[:], X0[:], bcast_mid(identb, 8), ALU.add)
                yield
                Xc, Yc, Qc = X0, Y0, Q0
                for kk in range(1, 6):
                    Xn, Yn = Xb[kk % 2], Yb[kk % 2]
                    Qn = Qb2[kk % 2] if kk < 5 else TT
                    for hb in range(2):
                        hs = range(4 * hb, 4 * hb + 4)
                        by, bx = PB[2 * hb], PB[2 * hb + 1]
                        for h in hs:
                            k.mm(by[:, (h % 4) * 128:(h % 4) * 128 + 128], Xc[:, h, :], Yc[:, h, :])
                        if kk < 5:
                            for h in hs:
                                k.mm(bx[:, (h % 4) * 128:(h % 4) * 128 + 128], Yc[:, h, :], Xc[:, h, :])
                        k.cp(Yn[:, 4 * hb:4 * hb + 4, :], by[:, :].re("p (a b) -> p a b", a=4), eng=("dve" if hb == 0 else "act"))
                        if kk < 5:
                            k.cp(Xn[:, 4 * hb:4 * hb + 4, :], bx[:, :].re("p (a b) -> p a b", a=4), eng=("act" if hb == 0 else "dve"))
                        for h in hs:
                            o_ = by[:, (h % 4) * 128:(h % 4) * 128 + 128]
                            k.mm(o_, identb, Qc[:, h, :], start=True, stop=False)
                            k.mm(o_, Yn[:, h, :], Qc[:, h, :], start=False, stop=True)
                        k.cp(Qn[:, 4 * hb:4 * hb + 4, :], by[:, :].re("p (a b) -> p a b", a=4), eng=("dve" if hb == 0 else "act"))
                        yield
                    Xc, Yc, Qc = Xn, Yn, Qn
                for h in range(8):
                    k.mm(PB[h // 4][:, (h % 4) * 128:(h % 4) * 128 + 128], kb_[:, h, :], TT[:, h, :])
                for hb in range(2):
                    k.ts(nwT[:, 4 * hb:4 * hb + 4, :], PB[hb][:, :].re("p (a b) -> p a b", a=4), -1.0, ALU.mult)
                yield

            def rec(j, t=t):
                bc0 = j * 128
                gblk = t * 4 + j
                sidx = gblk % 2
                gl, Re, attnT, TT = gl2[sidx], Re2[sidx], attnT2[sidx], TT2[sidx]
                vb_, kdec, nwT = vb2[sidx], kdec2[sidx], nwT2[sidx]
                for c in range(2):
                    rsl = slice(64 * c, 64 * c + 64)
                    for h in range(8):
                        o_ = PB[4 + h // 4][:, (h % 4) * 128:(h % 4) * 128 + 128]
                        k.mm(o_, TT[:, h, :], vb_[:, h, :], start=True, stop=False)
                        k.mm(o_, nwT[:, h, :], Sb[:, h, :], start=False, stop=True)
                    for hb in range(2):
                        k.cp(vn[rsl, 4 * hb:4 * hb + 4, :], PB[4 + hb][rsl, :].re("p (a b) -> p a b", a=4),
                             eng=("dve" if hb == 0 else "act"))
                    yield
                    for h in range(8):
                        k.mm(PB[6][:, h * 64:(h + 1) * 64], vn[rsl, h, :], attnT[rsl, h, 64 * c:64 * c + 64])
                    for h in range(8):
                        k.mm(PB[7][:, h * 64:(h + 1) * 64], Sb[:, h, :], qkT[:, h // 2, bc0 + 64 * c:bc0 + 64 * c + 64])
                    k.tt(t1[:], PB[7][:, :].re("p (a b) -> p a b", a=8), Re[:, :, 64 * c:64 * c + 64], ALU.mult)
                    k.tt(oT[:, :, bc0 + 64 * c:bc0 + 64 * c + 64], t1[:], PB[6][:, :].re("p (a b) -> p a b", a=8), ALU.add)
                    yield
                    for h in range(8):
                        k.mm(PB[4 + h // 4][:, (h % 4) * 128:(h % 4) * 128 + 128], kdec[rsl, h, :], vn[rsl, h, :])
                    k.tt(Sf[:], Sf[:], bcast_last(gl[c][:, :], 128), ALU.mult)
                    for hb in range(2):
                        k.tt(Sf[:, 4 * hb:4 * hb + 4, :], Sf[:, 4 * hb:4 * hb + 4, :],
                             PB[4 + hb][:, :].re("p (a b) -> p a b", a=4), ALU.add)
                    k.cp(Sb[:], Sf[:], eng="act")
                    yield

            def run_all(gen):
                for _ in gen:
                    pass

            def interleave(ga, gb, ratio=3):
                a_done = b_done = False
                while not (a_done and b_done):
                    if not a_done:
                        try:
                            next(ga)
                        except StopIteration:
                            a_done = True
                    for _ in range(ratio):
                        if not b_done:
                            try:
                                next(gb)
                            except StopIteration:
                                b_done = True

            run_all(pre(0))
            for j in range(4):
                if j < 3:
                    interleave(rec(j), pre(j + 1))
                else:
                    run_all(rec(j))
            for h in range(8):
                pb = PB[h % 4]
                k.act(sqc[:], oT[:, h, :], AF.Square)
                k.mm(pb[:, :], onesb, sqc[:])
                k.act(rs[:], pb[:, :], AF.Sqrt, bias=EPS, scale=1.0 / 128)
                k.recip(rs[:], rs[:])
                k.stt(cvo[:], oT[:, h, :], cnorm[:, 0:1], rs[:], ALU.mult, ALU.mult)
                pz = PB[4 + h % 4]
                cc = 16 + h
                for c in range(8):
                    k.mm(pz[:, :], W_C[:, c, cc * 128:(cc + 1) * 128], hT[:, c, :], start=(c == 0), stop=(c == 7))
                k.act(zs1[:], pz[:, :], AF.Silu, bias=biasC[:, cc:cc + 1])
                yo = ycT_[h % 2]
                k.tt(yo[:], cvo[:], zs1[:], ALU.mult)
                k.dma(ycT_d[h * 128:(h + 1) * 128, t * 512:(t + 1) * 512], yo[:])

    def post(l, xsrc, xdst):
        sink_e = rowbc(sinks[l:l + 1, :], 8, "sink_e")
        k.act(sink_e[:], sink_e[:], AF.Exp)
        gain2 = colvec(norm_ffn[l], 8, "gain2")
        biasG = colvec(b_in[l, 5392:8464], 24, "biasG")
        NWB = 6
        wb = [k.sb("wb%d" % i, [128, 8, 512], BF16) for i in range(NWB)]
        wcnt = [0]

        def wpanel_k8(dram2d, c0, ncol):
            w = wb[wcnt[0] % NWB]
            wcnt[0] += 1
            k.dma(w[:, :, 0:ncol], V(dram2d.t, dram2d.ap.rearrange("(c p) n -> p c n", p=128)[:, :, c0:c0 + ncol]))
            return w

        def wpanel_rows(dram2d, r0, nchunk, c0, ncol):
            w = wb[wcnt[0] % NWB]
            wcnt[0] += 1
            wv = w[:, :, :].re("p a b -> p (a b)")[:, 0:nchunk * ncol].re("p (a b) -> p a b", a=nchunk)
            k.dma(wv, V(dram2d.t, dram2d.ap[r0:r0 + nchunk * 128, :].rearrange("(c p) n -> p c n", p=128)[:, :, c0:c0 + ncol]))
            return wv

        xt_ = [k.sb("pxt%d" % i, [128, 8, 512], F32) for i in range(1)]
        hT_ = [k.sb("phT%d" % i, [128, 8, 512], BF16) for i in range(1)]
        ycT_ = [k.sb("pyc%d" % i, [128, 8, 512], BF16) for i in range(1)]
        yaT = k.sb("yaT", [128, 4, 512], BF16)
        ybT = k.sb("ybT", [128, 4, 512], BF16)
        Rl = [k.sb("Rl%d" % i, [128, 8, 65], F32) for i in range(4)]
        rden = k.sb("rden", [128, 8, 1], F32)
        ytok = k.sb("ytok", [128, 8, 64], BF16)
        sg = [k.sb("sg%d" % i, [128, 512], F32) for i in range(3)]
        mt = k.sb("mt", [128, 512], F32)
        mt2 = k.sb("mt2", [128, 512], F32)
        mrgT = k.sb("mrgT", [128, 8, 512], BF16)
        x1T = k.sb("x1T", [128, 8, 512], F32)
        sq = mrgT
        rstd = k.sb("prstd", [128, 512], F32)
        h2T = k.sb("h2T", [128, 8, 512], BF16)
        hid = k.sb("hid", [128, 32, 512], BF16)
        x2T = x1T
        Wl = Wb_in.h.ap()[l]
        for t in range(NT):
            xt, hT, ycT = xt_[0], hT_[0], ycT_[0]
            tsl = slice(t * 512, (t + 1) * 512)
            k.dma(xt[:], fm(xsrc)[:, :, tsl])
            k.dma(hT[:], fm(hT_d)[:, :, tsl])
            k.dma(ycT[:], fm(ycT_d)[:, :, tsl])
            for j in range(4):
                blk = t * 4 + j
                rsl = slice(blk * 128, (blk + 1) * 128)
                for i in range(4):
                    k.dma(Rl[i][:, :, :].re("p a b -> p (a b)"), R_d[i][rsl, :])
                k.tt(Rl[0][:], Rl[0][:], Rl[1][:], ALU.add)
                k.tt(Rl[0][:], Rl[0][:], Rl[2][:], ALU.add)
                k.recip(rden[:], Rl[0][:, :, 64:65])
                k.tt(ytok[:], Rl[0][:, :, 0:64], V(rden, rden[:, :, :].ap.to_broadcast([128, 8, 64])), ALU.mult)
                pa = pbf(6)
                for c in range(4):
                    k.tr(pa[:, c * 128:(c + 1) * 128], ytok[:, 2 * c:2 * c + 2, :].re("p a b -> p (a b)"), identb)
                k.cp(yaT[:, :, j * 128:(j + 1) * 128], pa[:, 0:512].re("p (a b) -> p a b", a=4))
                se = V(sink_e, sink_e[:, :].ap.rearrange("p (g j) -> p j g", g=2).unsqueeze(3))
                R3 = Rl[3][:, :, :].re("p (j g) e -> p j g e", g=2)
                k.tt(R3[:, :, :, 64:65], R3[:, :, :, 64:65], se, ALU.add)
                k.recip(rden[:], Rl[3][:, :, 64:65])
                ytv = ytok[:, :, :].re("p (g j) e -> p j g e", g=2)
                k.tt(ytv, R3[:, :, :, 0:64],
                     V(rden, rden[:, :, :].ap.rearrange("p (j g) o -> p j g o", g=2).to_broadcast([128, 4, 2, 64])), ALU.mult)
                pb_ = pbf(7)
                for c in range(4):
                    k.tr(pb_[:, c * 128:(c + 1) * 128], ytok[:, 2 * c:2 * c + 2, :].re("p a b -> p (a b)"), identb)
                k.cp(ybT[:, :, j * 128:(j + 1) * 128], pb_[:, 0:512].re("p (a b) -> p a b", a=4), eng="act")
            for mg in range(2):
                wg = [wpanel_k8(V(Wb_in, Wl), 5392 + gi_ * 1024 + mg * 512, 512) for gi_ in range(3)]
                wa = wpanel_rows(V(Wb_ba, Wb_ba.h.ap()[l]), 0, 4, mg * 512, 512)
                wbb = wpanel_rows(V(Wb_bb, Wb_bb.h.ap()[l]), 0, 4, mg * 512, 512)
                wc = wpanel_k8(V(Wb_bc, Wb_bc.h.ap()[l]), mg * 512, 512)
                for mm_ in range(4):
                    m = mg * 4 + mm_
                    cs_ = slice(mm_ * 128, (mm_ + 1) * 128)
                    for gi_ in range(3):
                        pb = PB[gi_]
                        for c in range(8):
                            k.mm(pb[:, :], wg[gi_][:, c, cs_], hT[:, c, :], start=(c == 0), stop=(c == 7))
                        k.act(sg[gi_][:], pb[:, :], AF.Sigmoid, bias=biasG[:, gi_ * 8 + m:gi_ * 8 + m + 1])
                    for c in range(4):
                        k.mm(PB[3][:, :], wa[:, c, cs_], yaT[:, c, :], start=(c == 0), stop=(c == 3))
                    k.tt(mt[:], sg[0][:], PB[3][:, :], ALU.mult)
                    for c in range(4):
                        k.mm(PB[4][:, :], wbb[:, c, cs_], ybT[:, c, :], start=(c == 0), stop=(c == 3))
                    k.tt(mt2[:], sg[1][:], PB[4][:, :], ALU.mult)
                    k.tt(mt[:], mt[:], mt2[:], ALU.add, eng="pool")
                    for c in range(8):
                        k.mm(PB[5][:, :], wc[:, c, cs_], ycT[:, c, :], start=(c == 0), stop=(c == 7))
                    k.tt(mt2[:], sg[2][:], PB[5][:, :], ALU.mult)
                    k.tt(mrgT[:, m, :], mt[:], mt2[:], ALU.add, eng="pool")
            for mg in range(2):
                wo = wpanel_k8(V(Wb_out, Wb_out.h.ap()[l]), mg * 512, 512)
                for mm_ in range(4):
                    m = mg * 4 + mm_
                    pb = PB[6 + m % 2]
                    for c in range(8):
                        k.mm(pb[:, :], wo[:, c, mm_ * 128:(mm_ + 1) * 128], mrgT[:, c, :], start=(c == 0), stop=(c == 7))
                    k.tt(x1T[:, m, :], xt[:, m, :], pb[:, :], ALU.add)
            rmsnorm_T(x1T, gain2, h2T, sq, rstd, PB[0])
            for fg in range(8):
                w1 = wpanel_k8(V(Wb_f1, Wb_f1.h.ap()[l]), fg * 512, 512)
                for ff in range(4):
                    f = fg * 4 + ff
                    pb = PB[1 + f % 3]
                    for c in range(8):
                        k.mm(pb[:, :], w1[:, c, ff * 128:(ff + 1) * 128], h2T[:, c, :], start=(c == 0), stop=(c == 7))
                    k.stt(hid[:, f, :], pb[:, :], 0.0, pb[:, :], ALU.max, ALU.mult) if False else None
                    k.act(mt[:] if f % 2 == 0 else mt2[:], pb[:, :], AF.Relu)
                    src_ = mt if f % 2 == 0 else mt2
                    k.tt(hid[:, f, :], src_[:], src_[:], ALU.mult, eng=("dve" if f % 2 == 0 else "pool"))
            for m in range(8):
                pb = PB[4 + m % 4]
                for fq in range(4):
                    w2 = wpanel_rows(V(Wb_f2, Wb_f2.h.ap()[l]), fq * 1024, 8, m * 128, 128)
                    for c in range(8):
                        f = fq * 8 + c
                        k.mm(pb[:, :], w2[:, c, :], hid[:, f, :], start=(f == 0), stop=(f == 31))
                k.tt(x2T[:, m, :], x1T[:, m, :], pb[:, :], ALU.add)
            k.dma(fm(xdst)[:, :, tsl], x2T[:])

    if "L0" in phases:
        layer(0, xT[0], xT[1])
    if "L1" in phases:
        layer(1, xT[1], xT[2])

    if "F" in phases:
        k.phase_reset()
        gainF = colvec(norm_final[:], 8, "gainF")
        xt_ = [k.sb("fxt%d" % i, [128, 8, 512], F32) for i in range(2)]
        sq = k.sb("fsq", [128, 8, 512], BF16)
        rstd = k.sb("frstd", [128, 512], F32)
        yT = k.sb("fyT", [128, 8, 512], F32)
        otok = [k.sb("otok%d" % i, [128, D], F32) for i in range(2)]
        src = xT[2] if "L1" in phases else (xT[1] if "L0" in phases else xT[0])
        for t in range(NT):
            xt = xt_[t % 2]
            k.dma(xt[:], fm(src)[:, :, t * 512:(t + 1) * 512])
            k.act(sq[:, 0:4, :], xt[:, 0:4, :], AF.Square)
            k.act(sq[:, 4:8, :], xt[:, 4:8, :], AF.Square)
            for c in range(8):
                k.mm(PB[7][:, :], onesb, sq[:, c, :], start=(c == 0), stop=(c == 7))
            k.act(rstd[:], PB[7][:, :], AF.Sqrt, bias=EPS, scale=1.0 / D)
            k.recip(rstd[:], rstd[:])
            for c in range(8):
                k.stt(yT[:, c, :], xt[:, c, :], gainF[:, c:c + 1], rstd[:], ALU.mult, ALU.mult)
            for j in range(4):
                ot = otok[j % 2]
                for c in range(8):
                    k.tr(PB[c // 4][:, (c % 4) * 128:(c % 4) * 128 + 128], yT[:, c, j * 128:(j + 1) * 128], identf)
                k.cp(ot[:, 0:512], PB[0][:, :], eng="dve")
                k.cp(ot[:, 512:1024], PB[1][:, :], eng="act")
                blk = t * 4 + j
                k.dma(out[blk * 128:(blk + 1) * 128, :], ot[:])
    return k


WEIGHT_KEYS = ["w_in", "b_in", "conv_w", "a_log", "dt_bias", "sinks", "c_norm", "w_branch_a", "w_branch_b",
               "w_branch_c", "w_out", "norm_mix", "norm_ffn", "w_ff1", "w_ff2", "norm_final"]


def kernel(**inputs):
    from concourse.bass_utils import run_bass_kernel_spmd
    x = np.ascontiguousarray(inputs["x"], dtype=np.float32)
    B, S, _ = x.shape
    pos = np.ascontiguousarray(inputs["positions"], dtype=np.int32)
    cf, cb = host_consts()
    k = build(S)
    nc = k.finish()
    wts = {kk: np.ascontiguousarray(inputs[kk], dtype=np.float32) for kk in WEIGHT_KEYS}
    in_maps = []
    for core in range(8):
        b = core % B
        m = {"x": x[b], "pos": pos[b], "cf": cf, "cb": cb}
        m.update(wts)
        in_maps.append(m)
    res = run_bass_kernel_spmd(nc, in_maps, core_ids=list(range(8)))
    return np.stack([res.results[b]["out"] for b in range(B)], axis=0).astype(np.float32)
```
